# Optimizing a Trainium2 kernel written in Bass

```python
import math
import jax, jax.numpy as jnp
from jax import lax
import numpy as np

D_MODEL = 2048
BATCH = 4
SEQ = 2048
DEPTH = 4

HEAD_DIM = 128
D_FF = 5632
MACARON_WEIGHT = 0.5
RMS_EPS = 1e-6
NEG_INF = -1e30

A_HEADS = 8
A_PATTERNS = ((128, 1), (512, 4), (2048, 16))
A_BLOCK = 64
B_HEADS = 8
B_KV_HEADS = 2
B_HALF_WINDOW = 128
B_BLOCK = 128
C_HEADS = 16
GRID_W = 64
C_KR_MAX = 8
C_KC = 16
C_QR_MAX = 8
C_QC = 16

AB_WIDTH = (A_HEADS + B_HEADS) * HEAD_DIM
AB_IN = 3 * A_HEADS * HEAD_DIM + B_HEADS * HEAD_DIM + 2 * B_KV_HEADS * HEAD_DIM
C_WIDTH = C_HEADS * HEAD_DIM
C_IN = 3 * C_WIDTH
N_EVEN = (DEPTH + 1) // 2
N_ODD = DEPTH // 2

kernel_name = "hybrid_dilated_banded_natten_macaron"


def _rmsnorm(x, g):
    x32 = x.astype(jnp.float32)
    y = x32 * lax.rsqrt(jnp.mean(x32 * x32, axis=-1, keepdims=True) + RMS_EPS)
    return (y * g.astype(jnp.float32)).astype(x.dtype)


def _swiglu(h, w_gate, w_up, w_down):
    return (jax.nn.silu(h @ w_gate) * (h @ w_up)) @ w_down


def _alibi_slopes(n):
    return 2.0 ** (-8.0 * jnp.arange(1, n + 1, dtype=jnp.float32) / n)


def _banded_attention(q, k, v, half_window, block, slopes, dist_scale, sink=None):
    L, dh = q.shape[-2], q.shape[-1]
    nb = L // block
    kw = block + 2 * half_window
    pad = [(0, 0)] * (k.ndim - 2) + [(half_window, half_window), (0, 0)]
    kp = jnp.pad(k, pad)
    vp = jnp.pad(v, pad)
    kidx = jnp.arange(nb)[:, None] * block + jnp.arange(kw)[None, :]
    kb = jnp.take(kp, kidx, axis=-2)
    vb = jnp.take(vp, kidx, axis=-2)
    qb = q.reshape(q.shape[:-2] + (nb, block, dh))
    s = jnp.einsum('...hgnqd,...hnkd->...hgnqk', qb, kb).astype(jnp.float32) * (dh ** -0.5)
    qpos = jnp.arange(L).reshape(nb, block)
    kpos = kidx - half_window
    dist = jnp.abs(qpos[:, :, None] - kpos[:, None, :])
    valid = (dist <= half_window) & (kpos >= 0)[:, None, :] & (kpos < L)[:, None, :]
    bias = -slopes[:, :, None, None, None] * (dist * dist_scale).astype(jnp.float32)
    s = jnp.where(valid, s + bias, NEG_INF)
    m = jnp.max(s, axis=-1)
    if sink is not None:
        sink32 = sink.astype(jnp.float32)[:, :, None, None]
        m = jnp.maximum(m, sink32)
    p = jnp.exp(s - m[..., None])
    denom = jnp.sum(p, axis=-1)
    if sink is not None:
        denom = denom + jnp.exp(sink32 - m)
    o = jnp.einsum('...hgnqk,...hnkd->...hgnqd', (p / denom[..., None]).astype(v.dtype), vb)
    lse = m + jnp.log(denom)
    return o.reshape(q.shape), lse.reshape(q.shape[:-1])


def _dilated_mixture(q, k, v, slopes):
    B, S, H, dh = q.shape
    outs, lses = [], []
    for window, r in A_PATTERNS:
        L = S // r
        def to_res(t):
            return t.reshape(B, L, r, H, dh).transpose(0, 2, 3, 1, 4)
        o, lse = _banded_attention(to_res(q)[:, :, :, None], to_res(k), to_res(v),
                                   window // (2 * r), math.gcd(L, A_BLOCK), slopes[:, None], r)
        outs.append(o[:, :, :, 0].transpose(0, 3, 1, 2, 4).reshape(B, S, H, dh))
        lses.append(lse[:, :, :, 0].transpose(0, 3, 1, 2).reshape(B, S, H))
    w = jax.nn.softmax(jnp.stack(lses), axis=0)
    out = jnp.sum(w[..., None] * jnp.stack(outs).astype(jnp.float32), axis=0)
    return out.astype(q.dtype)


def _windowed_gqa_sink(q, k, v, sink):
    B, S, hq, dh = q.shape
    grp = hq // B_KV_HEADS
    qg = q.reshape(B, S, B_KV_HEADS, grp, dh).transpose(0, 2, 3, 1, 4)
    kt = k.transpose(0, 2, 1, 3)
    vt = v.transpose(0, 2, 1, 3)
    slopes = _alibi_slopes(hq).reshape(B_KV_HEADS, grp)
    o, _ = _banded_attention(qg, kt, vt, B_HALF_WINDOW, B_BLOCK, slopes, 1,
                             sink.reshape(B_KV_HEADS, grp))
    return o.transpose(0, 3, 1, 2, 4).reshape(B, S, hq, dh)


def _neighbourhood_attention(q, k, v, rpb):
    B, S, H, dh = q.shape
    rows = S // GRID_W
    kr = min(C_KR_MAX, rows)
    qr = math.gcd(rows, C_QR_MAX)
    span_r = min(rows, kr + qr - 1)
    span_c = min(GRID_W, C_KC + C_QC - 1)
    nrb, ncb = rows // qr, GRID_W // C_QC
    qrow = jnp.arange(rows)
    qcol = jnp.arange(GRID_W)
    row_start = jnp.clip(qrow - kr // 2, 0, rows - kr)
    col_start = jnp.clip(qcol - C_KC // 2, 0, GRID_W - C_KC)
    rb_start = jnp.clip(jnp.arange(nrb) * qr - kr // 2, 0, rows - span_r)
    cb_start = jnp.clip(jnp.arange(ncb) * C_QC - C_KC // 2, 0, GRID_W - span_c)
    kr_idx = rb_start[:, None] + jnp.arange(span_r)[None, :]
    kc_idx = cb_start[:, None] + jnp.arange(span_c)[None, :]

    def gather_kv(t):
        g = jnp.take(t.reshape(B, rows, GRID_W, H, dh), kr_idx, axis=1)
        g = jnp.take(g, kc_idx, axis=3)
        return g.transpose(0, 5, 1, 3, 2, 4, 6).reshape(B, H, nrb, ncb, span_r * span_c, dh)

    kb = gather_kv(k)
    vb = gather_kv(v)
    qb = q.reshape(B, nrb, qr, ncb, C_QC, H, dh).transpose(0, 5, 1, 3, 2, 4, 6)
    qb = qb.reshape(B, H, nrb, ncb, qr * C_QC, dh)
    s = jnp.einsum('bhnmqd,bhnmkd->bhnmqk', qb, kb).astype(jnp.float32) * (dh ** -0.5)

    kr_b = kr_idx[:, None, :]
    rs = row_start.reshape(nrb, qr)[:, :, None]
    in_r = (kr_b >= rs) & (kr_b < rs + kr)
    dr = kr_b - qrow.reshape(nrb, qr)[:, :, None]
    kc_b = kc_idx[:, None, :]
    cs = col_start.reshape(ncb, C_QC)[:, :, None]
    in_c = (kc_b >= cs) & (kc_b < cs + C_KC)
    dc = kc_b - qcol.reshape(ncb, C_QC)[:, :, None]
    ri = jnp.clip(dr + C_KR_MAX - 1, 0, 2 * C_KR_MAX - 2)
    ci = jnp.clip(dc + C_KC - 1, 0, 2 * C_KC - 2)
    bias = rpb.astype(jnp.float32)[:, ri[:, :, :, None, None, None], ci[None, None, None]]
    mask = in_r[:, :, :, None, None, None] & in_c[None, None, None]
    bias = jnp.where(mask[None], bias, NEG_INF)
    bias = bias.transpose(0, 1, 4, 2, 5, 3, 6).reshape(H, nrb, ncb, qr * C_QC, span_r * span_c)
    p = jax.nn.softmax(s + bias[None], axis=-1)
    o = jnp.einsum('bhnmqk,bhnmkd->bhnmqd', p.astype(v.dtype), vb)
    o = o.reshape(B, H, nrb, ncb, qr, C_QC, dh).transpose(0, 2, 4, 3, 5, 1, 6)
    return o.reshape(B, S, H, dh)


def _mixer_ab(h, w_in, w_out, sink):
    B, S, _ = h.shape
    da, db, dkv = A_HEADS * HEAD_DIM, B_HEADS * HEAD_DIM, B_KV_HEADS * HEAD_DIM
    proj = h @ w_in
    qa, ka, va, qb, kb, vb = jnp.split(
        proj, [da, 2 * da, 3 * da, 3 * da + db, 3 * da + db + dkv], axis=-1)
    heads_a = lambda t: t.reshape(B, S, A_HEADS, HEAD_DIM)
    oa = _dilated_mixture(heads_a(qa), heads_a(ka), heads_a(va), _alibi_slopes(A_HEADS))
    ob = _windowed_gqa_sink(qb.reshape(B, S, B_HEADS, HEAD_DIM),
                            kb.reshape(B, S, B_KV_HEADS, HEAD_DIM),
                            vb.reshape(B, S, B_KV_HEADS, HEAD_DIM), sink)
    o = jnp.concatenate([oa.reshape(B, S, da), ob.reshape(B, S, db)], axis=-1)
    return o @ w_out


def _mixer_c(h, w_in, w_out, rpb):
    B, S, _ = h.shape
    q, k, v = jnp.split(h @ w_in, 3, axis=-1)
    heads = lambda t: t.reshape(B, S, C_HEADS, HEAD_DIM)
    o = _neighbourhood_attention(heads(q), heads(k), heads(v), rpb)
    return o.reshape(B, S, C_WIDTH) @ w_out


def setup_inputs(seed: int = 0) -> dict:
    key = jax.random.key(seed)
    ks = jax.random.split(key, 14)
    f32 = jnp.float32
    D, F = D_MODEL, D_FF
    nrm = lambda k, shape: jax.random.normal(k, shape, f32)
    return {
        'x': nrm(ks[0], (BATCH, SEQ, D)),
        'ffn_norm': 1.0 + 0.02 * nrm(ks[1], (DEPTH, 2, D)),
        'ffn_w_gate': nrm(ks[2], (DEPTH, 2, D, F)) * D ** -0.5,
        'ffn_w_up': nrm(ks[3], (DEPTH, 2, D, F)) * D ** -0.5,
        'ffn_w_down': nrm(ks[4], (DEPTH, 2, F, D)) * F ** -0.5,
        'mix_norm': 1.0 + 0.02 * nrm(ks[5], (DEPTH, D)),
        'ab_w_in': nrm(ks[6], (N_EVEN, D, AB_IN)) * D ** -0.5,
        'ab_w_out': nrm(ks[7], (N_EVEN, AB_WIDTH, D)) * AB_WIDTH ** -0.5,
        'ab_sink': 0.5 * nrm(ks[8], (N_EVEN, B_HEADS)),
        'c_w_in': nrm(ks[9], (N_ODD, D, C_IN)) * D ** -0.5,
        'c_w_out': nrm(ks[10], (N_ODD, C_WIDTH, D)) * C_WIDTH ** -0.5,
        'c_rpb': 0.1 * nrm(ks[11], (N_ODD, C_HEADS, 2 * C_KR_MAX - 1, 2 * C_KC - 1)),
        'final_norm': 1.0 + 0.02 * nrm(ks[12], (D,)),
    }


def reference(x, ffn_norm, ffn_w_gate, ffn_w_up, ffn_w_down, mix_norm, ab_w_in, ab_w_out,
              ab_sink, c_w_in, c_w_out, c_rpb, final_norm):
    for layer in range(DEPTH):
        h = _rmsnorm(x, ffn_norm[layer, 0])
        x = x + MACARON_WEIGHT * _swiglu(h, ffn_w_gate[layer, 0], ffn_w_up[layer, 0], ffn_w_down[layer, 0])
        h = _rmsnorm(x, mix_norm[layer])
        i = layer // 2
        if layer % 2 == 0:
            x = x + _mixer_ab(h, ab_w_in[i], ab_w_out[i], ab_sink[i])
        else:
            x = x + _mixer_c(h, c_w_in[i], c_w_out[i], c_rpb[i])
        h = _rmsnorm(x, ffn_norm[layer, 1])
        x = x + MACARON_WEIGHT * _swiglu(h, ffn_w_gate[layer, 1], ffn_w_up[layer, 1], ffn_w_down[layer, 1])
    return _rmsnorm(x, final_norm)
```

```python
import math
from contextlib import ExitStack

import numpy as np
import concourse.bass as bass
import concourse.mybir as mybir
from concourse.bass_utils import run_bass_kernel_spmd

F32 = mybir.dt.float32
BF16 = mybir.dt.bfloat16
ALU = mybir.AluOpType
AF = mybir.ActivationFunctionType

D_MODEL = 2048
BATCH = 4
SEQ = 2048
DEPTH = 4
HEAD_DIM = 128
D_FF = 5632
RMS_EPS = 1e-6
NCORES = 8
T = 1024
KD = 16
NFC = D_FF // 128
NEG = -30000.0


class Buf:
    __slots__ = ("ap", "keys")

    def __init__(self, ap, keys):
        self.ap = ap
        self.keys = keys


class Op:
    __slots__ = ("eng", "fn", "deps", "is_dma", "sem", "val", "signal", "extra_waits")

    def __init__(self, eng, fn, is_dma):
        self.eng = eng
        self.fn = fn
        self.deps = []
        self.is_dma = is_dma
        self.sem = None
        self.val = 0
        self.signal = False
        self.extra_waits = []


GRAN = 512
ENGS = ("pe", "act", "dve", "pool", "sp")
NDMASEM = 20


class Prog:
    def __init__(self, nc, es, sb_bytes):
        self.nc = nc
        self.es = es
        self.dry = False
        self.q = {e: [] for e in ENGS}
        self.last_w = {}
        self.readers = {}
        self.big = es.enter_context(nc.sbuf_tensor("big", [128, sb_bytes // 2], BF16))
        self.ps = es.enter_context(nc.psum_tensor("ps", [128, 4096], F32))
        self.eng_sem = {e: es.enter_context(nc.semaphore("s_" + e)) for e in ("pe", "act", "dve", "pool")}
        self.dma_sems = {q: [[es.enter_context(nc.semaphore("d_%s%d" % (q, i))), 0] for i in range(NDMASEM)]
                         for q in ("sp", "pool")}
        self.dma_rr = {"sp": 0, "pool": 0}
        self.cc_sem = es.enter_context(nc.semaphore("cc"))
        self.cc_cnt = 0

    def sb(self, off, nbytes, dtype=BF16, pat=None, **kw):
        assert off % 4 == 0 and nbytes % 4 == 0
        ap = self.big[:, off // 2:(off + nbytes) // 2]
        if dtype != BF16:
            ap = ap.bitcast(dtype)
        if pat is not None:
            ap = ap.rearrange(pat, **kw)
        keys = [("sb", g) for g in range(off // GRAN, (off + nbytes + GRAN - 1) // GRAN)]
        return Buf(ap, keys)

    def bank(self, b):
        return Buf(self.ps[:, b * 512:(b + 1) * 512], [("ps", b)])

    def _track(self, op, reads, writes):
        deps = {}
        lw, rd = self.last_w, self.readers
        for b in reads:
            for k in b.keys:
                w = lw.get(k)
                if w is not None:
                    deps[id(w)] = w
        for b in writes:
            for k in b.keys:
                w = lw.get(k)
                if w is not None:
                    deps[id(w)] = w
                r = rd.get(k)
                if r:
                    for o in r.values():
                        deps[id(o)] = o
        deps.pop(id(op), None)
        for b in reads:
            for k in b.keys:
                r = rd.get(k)
                if r is None:
                    r = rd[k] = {}
                r[id(op) if op.is_dma else op.eng] = op
        for b in writes:
            for k in b.keys:
                lw[k] = op
                rd[k] = {}
        for o in deps.values():
            if o.eng == "pe" and op.eng == "pe" and not o.is_dma and not op.is_dma:
                continue
            op.deps.append(o)
            o.signal = True

    def _add(self, eng, fn, reads, writes, is_dma=False):
        if self.dry:
            return None
        op = Op(eng, fn, is_dma)
        self._track(op, reads, writes)
        self.q[eng].append(op)
        return op

    def pe(self, fn, reads, writes):
        return self._add("pe", fn, reads, writes)

    def act(self, fn, reads, writes):
        return self._add("act", fn, reads, writes)

    def dve(self, fn, reads, writes):
        return self._add("dve", fn, reads, writes)

    def poolc(self, fn, reads, writes):
        return self._add("pool", fn, reads, writes)

    def dma(self, queue, out_ap, in_ap, reads, writes):
        if self.dry:
            return None
        op = self._add(queue, lambda e: e.dma_start(out=out_ap, in_=in_ap), reads, writes, is_dma=True)
        pool = self.dma_sems[queue]
        i = self.dma_rr[queue]
        self.dma_rr[queue] = (i + 1) % NDMASEM
        ent = pool[i]
        if ent[1] > 0:
            op.extra_waits.append((ent[0], ent[1]))
        ent[1] += 16
        op.sem, op.val = ent[0], ent[1]
        op.signal = True
        return op

    def collective(self, kind, in_ap, out_ap, groups, reads, writes):
        if self.dry:
            return None

        def fn(e):
            return e.collective_compute(kind, ALU.bypass, replica_groups=groups,
                                        ins=[in_ap], outs=[out_ap])
        op = self._add("pool", fn, reads, writes, is_dma=True)
        self.cc_cnt += 1
        op.sem, op.val = self.cc_sem, self.cc_cnt
        op.signal = True
        return op

    def finalize(self, block):
        for e in ("pe", "act", "dve", "pool"):
            n = 0
            for op in self.q[e]:
                if op.signal and not op.is_dma:
                    n += 1
                    op.sem, op.val = self.eng_sem[e], n
        engobj = {"pe": "tensor", "act": "scalar", "dve": "vector", "pool": "gpsimd", "sp": "sync"}

        def run(e, name):
            waited = {}
            for op in self.q[name]:
                need = {}
                for d in op.deps:
                    s = d.sem
                    if need.get(id(s), (None, 0))[1] < d.val:
                        need[id(s)] = (s, d.val)
                for (s, v) in op.extra_waits:
                    if need.get(id(s), (None, 0))[1] < v:
                        need[id(s)] = (s, v)
                for k, (s, v) in need.items():
                    if waited.get(k, 0) < v:
                        e.wait_ge(s, v)
                        waited[k] = v
                ins = op.fn(e)
                if op.signal:
                    if op.is_dma and op.sem is not self.cc_sem:
                        ins.then_inc(op.sem, 16)
                    else:
                        ins.then_inc(op.sem, 1)
            if name in self.dma_sems:
                for s, v in self.dma_sems[name]:
                    if v > 0 and waited.get(id(s), 0) < v:
                        e.wait_ge(s, v)

        for name in ENGS:
            if not self.q[name]:
                continue
            getattr(block, engobj[name])(lambda e, name=name: run(e, name))


class WStream:
    def __init__(self, P, base, nslots, slot_bytes):
        self.P = P
        self.base = base
        self.ns = nslots
        self.sbytes = slot_bytes
        self.plan = []
        self.issued = 0
        self.cursor = 0
        self.bufs = {}

    def get(self, src):
        P = self.P
        if P.dry:
            self.plan.append(src)
            return Buf(None, [])
        i = self.cursor
        self.cursor += 1
        while self.issued < min(len(self.plan), i + self.ns - 1):
            self._issue(self.issued)
            self.issued += 1
        return self.bufs.pop(i)

    def _issue(self, i):
        P = self.P
        src = self.plan[i]
        shp = list(src.shape)
        n = 1
        for s in shp[1:]:
            n *= s
        assert n * 2 <= self.sbytes, (shp, self.sbytes)
        off = self.base + (i % self.ns) * self.sbytes
        if len(shp) == 3:
            b = P.sb(off, n * 2, BF16, "p (a b) -> p a b", a=shp[1])
        else:
            b = P.sb(off, n * 2, BF16)
        P.dma("pool", b.ap, src, [], [b])
        self.bufs[i] = b


XT_OFF = 0
HT_OFF = 65536
CONST_OFF = 98304
W_OFF = 100352
W_SLOTS = 6
W_SLOT_BYTES = 8192
R_OFF = W_OFF + W_SLOTS * W_SLOT_BYTES
SB_BYTES = 204800
R_BYTES = SB_BYTES - R_OFF

N_GAIN = DEPTH * 3 + 1


class Ctx:
    pass


def setup_common(P, C):
    C.x = [[P.sb(XT_OFF + (kd * T + th * 512) * 4, 2048, F32) for th in range(2)] for kd in range(KD)]
    C.xk = [P.sb(XT_OFF + kd * T * 4, 4096, F32) for kd in range(KD)]
    C.h = [[P.sb(HT_OFF + (kd * T + th * 512) * 2, 1024, BF16) for th in range(2)] for kd in range(KD)]
    C.hk = [P.sb(HT_OFF + kd * T * 2, 2048, BF16) for kd in range(KD)]
    C.ones_f = P.sb(CONST_OFF, 512, F32)
    C.ones_b = P.sb(CONST_OFF + 512, 256, BF16)
    C.gains = P.sb(CONST_OFF + 768, N_GAIN * KD * 4, F32)
    C.sink = P.sb(CONST_OFF + 768 + 832, 64, F32)
    C.W = WStream(P, W_OFF, W_SLOTS, W_SLOT_BYTES)


def emit_consts(P, C, D):
    P.dve(lambda e: e.memset(C.ones_f.ap, 1.0 / D_MODEL), [], [C.ones_f])
    P.dve(lambda e: e.memset(C.ones_b.ap, 1.0), [], [C.ones_b])
    P.dma("sp", C.gains.ap, D["gains"], [], [C.gains])
    P.dma("sp", C.sink.ap, D["sink"], [], [C.sink])
    P.act(lambda e: e.activation(out=C.sink.ap, in_=C.sink.ap, func=AF.Exp), [C.sink], [C.sink])


def emit_load_x(P, C, src):
    v = src.rearrange("(kd p) t -> p kd t", p=128)
    for kd in range(KD):
        P.dma("sp", C.xk[kd].ap, v[:, kd, :], [], [C.xk[kd]])


def emit_store_x(P, C, dst):
    v = dst.rearrange("(kd p) t -> p kd t", p=128)
    for kd in range(KD):
        P.dma("sp", v[:, kd, :], C.xk[kd].ap, [C.xk[kd]], [])


def emit_rmsnorm(P, C, gidx, final=False):
    sqb = [P.sb(R_OFF + 36864 + i * 1024, 1024, BF16) for i in range(4)]
    rstd = [P.sb(R_OFF + 40960 + i * 2048, 2048, F32) for i in range(2)]
    for th in range(2):
        bk = P.bank(th)
        for kd in range(KD):
            xb = C.x[kd][th]
            s = sqb[(th * KD + kd) % 4]
            P.act(lambda e, xb=xb, s=s: e.activation(out=s.ap, in_=xb.ap, func=AF.Square), [xb], [s])
            P.pe(lambda e, bk=bk, s=s, kd=kd: e.matmul(bk.ap, lhsT=C.ones_b.ap, rhs=s.ap, start=(kd == 0),
                                                       stop=(kd == KD - 1)), [C.ones_b, s], [bk])
        r = rstd[th]
        P.act(lambda e, bk=bk, r=r: e.activation(out=r.ap, in_=bk.ap, func=AF.Ln, bias=RMS_EPS,
                                                 scale=1.0 / D_MODEL), [bk], [r])
        P.act(lambda e, r=r: e.activation(out=r.ap, in_=r.ap, func=AF.Exp, scale=-0.5), [r], [r])
        for kd in range(KD):
            xb = C.x[kd][th]
            g = C.gains.ap[:, gidx * KD + kd:gidx * KD + kd + 1]
            out = xb if final else C.h[kd][th]
            P.dve(lambda e, xb=xb, g=g, out=out, r=r: e.scalar_tensor_tensor(
                out=out.ap, in0=xb.ap, scalar=g, in1=r.ap, op0=ALU.mult, op1=ALU.mult), [xb, C.gains, r], [out])


def emit_ffn(P, C, D, l, j):
    emit_rmsnorm(P, C, l * 3 + (0 if j == 0 else 2))
    Wg = D["ffn_w_gate"][l, j].rearrange("(kd p) f -> p kd f", p=128)
    Wu = D["ffn_w_up"][l, j].rearrange("(kd p) f -> p kd f", p=128)
    Wd = D["ffn_w_down"][l, j].rearrange("(c p) d -> p c d", p=128)
    act = [[[P.sb(R_OFF + (s * 4 + c) * 2048 + th * 1024, 1024, BF16) for th in range(2)] for c in range(4)]
           for s in range(2)]
    sil = [P.sb(R_OFF + 16384 + i * 2048, 2048, F32) for i in range(2)]
    NG = NFC // 4
    st = {"gu": 0, "d": 0, "sil": 0}

    def GU(g):
        for hh in range(2):
            hg = 2 * g + hh
            wg = C.W.get(Wg[:, :, hg * 256:(hg + 1) * 256])
            wu = C.W.get(Wu[:, :, hg * 256:(hg + 1) * 256])
            for cc in range(2):
                c = hh * 2 + cc
                for th in range(2):
                    pb = (st["gu"] % 2) * 2
                    st["gu"] += 1
                    bG, bU = P.bank(pb), P.bank(pb + 1)
                    for (w, bk) in ((wg, bG), (wu, bU)):
                        for kd in range(KD):
                            hb = C.h[kd][th]
                            P.pe(lambda e, w=w, bk=bk, kd=kd, hb=hb, cc=cc: e.matmul(
                                bk.ap, lhsT=w.ap[:, kd, cc * 128:(cc + 1) * 128], rhs=hb.ap,
                                start=(kd == 0), stop=(kd == KD - 1)), [w, hb], [bk])
                    s = sil[st["sil"] % 2]
                    st["sil"] += 1
                    a = act[g % 2][c][th]
                    P.act(lambda e, s=s, bG=bG: e.activation(out=s.ap, in_=bG.ap, func=AF.Silu), [bG], [s])
                    P.dve(lambda e, s=s, bU=bU, a=a: e.tensor_tensor(out=a.ap, in0=s.ap, in1=bU.ap, op=ALU.mult),
                          [s, bU], [a])

    def DN(g):
        wd = [C.W.get(Wd[:, 4 * g + 2 * i:4 * g + 2 * i + 2, :]) for i in range(2)]
        for th in range(2):
            for do in range(KD):
                bk = P.bank(4 + st["d"] % 4)
                st["d"] += 1
                for c in range(4):
                    w = wd[c // 2]
                    a = act[g % 2][c][th]
                    P.pe(lambda e, w=w, bk=bk, c=c, a=a, do=do: e.matmul(
                        bk.ap, lhsT=w.ap[:, c % 2, do * 128:(do + 1) * 128], rhs=a.ap,
                        start=(c == 0), stop=(c == 3)), [w, a], [bk])
                xb = C.x[do][th]
                P.dve(lambda e, bk=bk, xb=xb: e.scalar_tensor_tensor(out=xb.ap, in0=bk.ap, scalar=0.5, in1=xb.ap,
                                                                     op0=ALU.mult, op1=ALU.add), [bk, xb], [xb])

    GU(0)
    for g in range(NG):
        if g + 1 < NG:
            GU(g + 1)
        DN(g)


QT_OFF = R_OFF
E_OFF = R_OFF + 32768
PT_OFF = R_OFF + 40960
RD_OFF = R_OFF + 45056
KST_OFF = R_OFF + 49152
VST_OFF = R_OFF + 32768
KSEQ_OFF = HT_OFF
VSEQ_OFF = HT_OFF + 8192
MSK_OFF = HT_OFF + 16384
SCALE = HEAD_DIM ** -0.5

A_TAB = 2944
DBG = {}
POOL_EVERY = 0
MIX = {
    "ab": dict(nkv=10, nq=16, ncol=4608),
    "c": dict(nkv=16, nq=16, ncol=6144),
}


def dbuf(name, *idx):
    return Buf(None, [("d", name) + tuple(idx)])


def proj_fm(P, C, Wv, col0, ncols, src, evac, st):
    for t in range(ncols // 256):
        w = C.W.get(Wv[:, :, col0 + t * 256:col0 + (t + 1) * 256])
        for cc in range(2):
            for th in range(2):
                bk = P.bank(st["b"] % 8)
                st["b"] += 1
                for kd in range(KD):
                    sb_ = src[kd][th]
                    P.pe(lambda e, w=w, bk=bk, kd=kd, sb_=sb_, cc=cc: e.matmul(
                        bk.ap, lhsT=w.ap[:, kd, cc * 128:(cc + 1) * 128], rhs=sb_.ap,
                        start=(kd == 0), stop=(kd == KD - 1)), [w, sb_], [bk])
                evac(t * 2 + cc, th, bk)


def emit_mixer(P, C, D, l, groups):
    kind = "ab" if l % 2 == 0 else "c"
    li = l // 2
    M = MIX[kind]
    nkv = M["nkv"]
    R = 2 * nkv * 128
    emit_rmsnorm(P, C, l * 3 + 1)
    Win = D["ab_w_in" if kind == "ab" else "c_w_in"][li].rearrange("(kd p) f -> p kd f", p=128)
    Wout = D["ab_w_out" if kind == "ab" else "c_w_out"][li].rearrange("(c p) d -> p c d", p=128)
    kv = D["kv%d" % l]
    kvf = D["kvf%d" % l]
    q = [[P.sb(QT_OFF + (h * T + qc * 512) * 2, 1024, BF16) for qc in range(2)] for h in range(16)]
    st = {"b": 0, "ev": 0}

    def kv_stored(blk):
        if (blk + 1) % 4 == 0 and not DBG.get("noag"):
            c = blk // 4
            rd_ = Buf(None, [("d", "kv", l, i) for i in range(4 * c, 4 * c + 4)])
            wr_ = Buf(None, [("d", "kvf", l, r, i) for r in range(2) for i in range(4 * c, 4 * c + 4)])
            P.collective("AllGather", kv[c * 512:(c + 1) * 512, :].opt(), kvf[c * 1024:(c + 1) * 1024, :].opt(),
                         groups, [rd_], [wr_])

    def copy_evac(out_buf, out_ap, bk, in_ap=None):
        in_ap = bk.ap if in_ap is None else in_ap
        st["ev"] += 1
        if st["ev"] % 2 == 0:
            P.act(lambda e: e.copy(out=out_ap, in_=in_ap), [bk], [out_buf])
        else:
            P.dve(lambda e: e.tensor_copy(out=out_ap, in_=in_ap), [bk], [out_buf])

    kst = [P.sb(KST_OFF + i * 2048, 2048, BF16) for i in range(2)]
    kcnt = {"n": 0}

    def k_proj(col0, ncols, head0):
        def evac(j, th, bk):
            ks = kst[(kcnt["n"] // 2) % 2]
            kcnt["n"] += 1
            copy_evac(ks, ks.ap[:, th * 512:(th + 1) * 512], bk)
            if th == 1:
                hd = head0 + j
                P.dma("sp", kv[hd * 128:(hd + 1) * 128, :], ks.ap, [ks], [dbuf("kv", l, hd)])
                kv_stored(hd)
        proj_fm(P, C, Win, col0, ncols, C.h, evac, st)

    vst = P.sb(VST_OFF, 8192, BF16, "p (j b d) -> p j b d", j=4, b=8)

    def v_proj(col0, ncols, head0):
        c = 0
        while c < ncols:
            gw = min(512, ncols - c)
            nh = gw // 128
            a0 = col0 + c
            if gw == 512:
                ws = [C.W.get(Win[:, 0:8, a0:a0 + 512]), C.W.get(Win[:, 8:16, a0:a0 + 512])]
                wsel = lambda kd, ws=ws: (ws[kd // 8], kd % 8)
            else:
                w = C.W.get(Win[:, :, a0:a0 + gw])
                wsel = lambda kd, w=w: (w, kd)
            for tb in range(8):
                bk = P.bank(st["b"] % 8)
                st["b"] += 1
                for kd in range(KD):
                    hb = C.hk[kd]
                    wb, wi = wsel(kd)
                    P.pe(lambda e, bk=bk, kd=kd, hb=hb, tb=tb, wb=wb, wi=wi, gw=gw: e.matmul(
                        bk.ap[:, 0:gw], lhsT=hb.ap[:, tb * 128:(tb + 1) * 128], rhs=wb.ap[:, wi, :],
                        start=(kd == 0), stop=(kd == KD - 1)), [wb, hb], [bk])
                copy_evac(vst, vst.ap[:, 0:nh, tb, :], bk, bk.ap[:, 0:gw].rearrange("p (j d) -> p j d", j=nh))
            for j in range(nh):
                hd = head0 + c // 128 + j
                r0 = (nkv + hd) * 128
                P.dma("sp", kv[r0:r0 + 128, :].rearrange("p (b d) -> p b d", b=8), vst.ap[:, j, :, :],
                      [vst], [dbuf("kv", l, nkv + hd)])
                kv_stored(nkv + hd)
            c += gw

    def q_proj(col0, ncols, head0):
        def evac(j, th, bk):
            qb = q[head0 + j][th]
            copy_evac(qb, qb.ap, bk)
        proj_fm(P, C, Win, col0, ncols, C.h, evac, st)

    if kind == "ab":
        k_proj(1024, 1024, 0)
        k_proj(4096, 256, 8)
        v_proj(2048, 1024, 0)
        v_proj(4352, 256, 8)
    else:
        k_proj(2048, 2048, 0)
        v_proj(4096, 2048, 0)
    if DBG.get("stop_after_ag"):
        return
    if kind == "ab":
        q_proj(0, 1024, 0)
        q_proj(3072, 1024, 8)
    else:
        q_proj(0, 2048, 0)

    if DBG.get("stop_after_q"):
        return
    kseq = [P.sb(KSEQ_OFF + i * 4096, 4096, BF16) for i in range(2)]
    vseq = [P.sb(VSEQ_OFF + i * 4096, 4096, BF16, "p (b d) -> p b d", d=128) for i in range(2)]
    msk = [P.sb(MSK_OFF + i * 8192, 8192, F32) for i in range(2)]
    Tb = [P.sb(E_OFF + i * 2048, 2048, F32) for i in range(4)]
    Pt = [P.sb(PT_OFF + i * 1024, 1024, BF16) for i in range(4)]
    rd = [P.sb(RD_OFF + i * 2048, 2048, F32) for i in range(2)]
    cnt = {"kv": 0, "m": 0, "s": 0, "e": 0, "p": 0, "o": 0}

    def kvf_rows(r, i):
        r0 = ((i // 4) * 2 + r) * 512 + (i % 4) * 128
        return kvf[r0:r0 + 128, :]

    def load_kv(kvh, mode):
        s = cnt["kv"] % 2
        cnt["kv"] += 1
        ks, vs = kseq[s], vseq[s]
        if mode == "full":
            for r in range(2):
                P.dma("sp", ks.ap[:, r * 1024:(r + 1) * 1024], kvf_rows(r, kvh), [dbuf("kvf", l, r, kvh)], [ks])
                P.dma("sp", vs.ap[:, r * 8:(r + 1) * 8, :], kvf_rows(r, nkv + kvh).rearrange("p (b d) -> p b d", d=128),
                      [dbuf("kvf", l, r, nkv + kvh)], [vs])
        else:
            hb = mode
            hw = hb * 128
            P.dma("sp", ks.ap[:, 0:hw], kvf_rows(0, kvh)[:, 1024 - hw:1024], [dbuf("kvf", l, 0, kvh)], [ks])
            P.dma("sp", ks.ap[:, hw:hw + 1024], kv[kvh * 128:(kvh + 1) * 128, :], [dbuf("kv", l, kvh)], [ks])
            P.dma("sp", ks.ap[:, hw + 1024:2 * hw + 1024], kvf_rows(1, kvh)[:, 0:hw], [dbuf("kvf", l, 1, kvh)], [ks])
            vv = lambda ap: ap.rearrange("p (b d) -> p b d", d=128)
            P.dma("sp", vs.ap[:, 0:hb, :], vv(kvf_rows(0, nkv + kvh))[:, 8 - hb:8, :], [dbuf("kvf", l, 0, nkv + kvh)], [vs])
            r0 = (nkv + kvh) * 128
            P.dma("sp", vs.ap[:, hb:hb + 8, :], vv(kv[r0:r0 + 128, :]), [dbuf("kv", l, nkv + kvh)], [vs])
            P.dma("sp", vs.ap[:, hb + 8:2 * hb + 8, :], vv(kvf_rows(1, nkv + kvh))[:, 0:hb, :], [dbuf("kvf", l, 1, nkv + kvh)], [vs])
        return ks, vs

    jobs = []
    kvgroups = []

    def add_job(hq, qc, blocks, kvg, sink_col, mgroups):
        jobs.append(dict(hq=hq, qc=qc, blocks=blocks, kvg=kvg, sink=sink_col, mg=mgroups))

    if kind == "ab":
        tA, tB = D["tabA"], D["tabB"]
        for h in range(8):
            kvgroups.append((h, "full"))
            for qc in range(2):
                mg = []
                for g in range(2):
                    lo = qc * 512 - (8 * g + 7) * 128 + 1920
                    offs = {i: (qc * 512 - i * 128 + 1920) - lo for i in range(8 * g, 8 * g + 8)}
                    mg.append((8 * g, tA[:, h * A_TAB + lo:h * A_TAB + lo + 1408], 1408, offs))
                add_job(h, qc, list(range(16)), len(kvgroups) - 1, None, mg)
        for hb in range(8):
            if hb % 4 == 0:
                kvgroups.append((8 + hb // 4, 1))
            for qc in range(2):
                c0 = (hb * 2 + qc) * 3072
                mg = [(3 * g, tB[:, c0 + g * 1536:c0 + (g + 1) * 1536], 1536,
                       {3 * g + k: k * 512 for k in range(3)}) for g in range(2)]
                add_job(8 + hb, qc, [qc * 4 + i for i in range(6)], len(kvgroups) - 1, li * 8 + hb, mg)
    else:
        tC = D["tabC%d" % li]
        for h in range(16):
            kvgroups.append((h, 2))
            for qc in range(2):
                c0 = (h * 2 + qc) * 4096
                mg = [(4 * g, tC[:, c0 + g * 2048:c0 + (g + 1) * 2048], 2048,
                       {4 * g + k: k * 512 for k in range(4)}) for g in range(2)]
                add_job(h, qc, [qc * 4 + i for i in range(8)], len(kvgroups) - 1, None, mg)

    kvres = {}
    kvfirst = {}
    for j in jobs:
        kvfirst.setdefault(j["kvg"], id(j))
    kvfirst = {v: k for k, v in kvfirst.items()}

    def load_kvgroup(k):
        kvres[k] = load_kv(*kvgroups[k])

    tasks = [(j, i) for j in jobs for i in range(len(j["blocks"]))]
    sbank = {}
    allg = [(j, g) for j in jobs for g in j["mg"]]
    gstart = {}
    for k, (j, g) in enumerate(allg):
        gstart[(id(j), g[0])] = k

    def load_group(k):
        j, (i0_, src, n, offs) = allg[k]
        m = msk[k % 2]
        P.dma("sp", m.ap[:, 0:n], src, [], [m])
        mt = j.setdefault("mt", {})
        for i, off in offs.items():
            mt[i] = (m, off)

    def S(t):
        j, i = tasks[t]
        if i == 0:
            j["ks"], j["vs"] = kvres[j["kvg"]]
            par = cnt["o"] % 2
            cnt["o"] += 1
            j["bo"], j["bd"], j["r"] = P.bank(4 + par), P.bank(6), rd[par]
        bk = P.bank((0, 1, 2, 3, 7)[cnt["s"] % 5])
        cnt["s"] += 1
        sbank[t] = bk
        b = j["blocks"][i]
        ks, qb = j["ks"], q[j["hq"]][j["qc"]]
        P.pe(lambda e: e.matmul(bk.ap, lhsT=ks.ap[:, b * 128:(b + 1) * 128], rhs=qb.ap, start=True, stop=True),
             [ks, qb], [bk])

    def PV(t):
        j, i = tasks[t]
        nb = len(j["blocks"])
        k = gstart.get((id(j), i))
        if k is not None and k + 1 < len(allg):
            load_group(k + 1)
        if i == 0 and id(j) in kvfirst and kvfirst[id(j)] + 1 < len(kvgroups):
            load_kvgroup(kvfirst[id(j)] + 1)
        bk = sbank.pop(t)
        b = j["blocks"][i]
        tb = Tb[cnt["e"] % 4]
        cnt["e"] += 1
        pt = Pt[cnt["p"] % 4]
        cnt["p"] += 1
        mb, off = j["mt"][i]
        vs, bo, bd = j["vs"], j["bo"], j["bd"]
        P.dve(lambda e: e.scalar_tensor_tensor(out=tb.ap, in0=bk.ap, scalar=SCALE, in1=mb.ap[:, off:off + 512],
                                               op0=ALU.mult, op1=ALU.add), [bk, mb], [tb])
        P.act(lambda e: e.activation(out=pt.ap, in_=tb.ap, func=AF.Exp), [tb], [pt])
        P.pe(lambda e: e.matmul(bo.ap, lhsT=vs.ap[:, b, :], rhs=pt.ap, start=(i == 0), stop=(i == nb - 1)),
             [vs, pt], [bo])
        P.pe(lambda e: e.matmul(bd.ap, lhsT=C.ones_b.ap, rhs=pt.ap, start=(i == 0), stop=(i == nb - 1)),
             [C.ones_b, pt], [bd])
        if i == nb - 1:
            r, qb = j["r"], q[j["hq"]][j["qc"]]
            if j["sink"] is None:
                P.act(lambda e: e.activation(out=r.ap, in_=bd.ap, func=AF.Ln), [bd], [r])
            else:
                sc = C.sink.ap[:, j["sink"]:j["sink"] + 1]
                P.act(lambda e: e.activation(out=r.ap, in_=bd.ap, func=AF.Ln, bias=sc, scale=1.0), [bd, C.sink], [r])
            P.act(lambda e: e.activation(out=r.ap, in_=r.ap, func=AF.Exp, scale=-1.0), [r], [r])
            P.dve(lambda e: e.tensor_tensor(out=qb.ap, in0=bo.ap, in1=r.ap, op=ALU.mult), [bo, r], [qb])

    LA = 4
    load_kvgroup(0)
    load_group(0)
    for t in range(min(LA, len(tasks))):
        S(t)
    for t in range(len(tasks)):
        if t + LA < len(tasks):
            S(t + LA)
        PV(t)

    def o_evac(j, th, bk):
        xb = C.x[j][th]
        P.dve(lambda e: e.tensor_tensor(out=xb.ap, in0=bk.ap, in1=xb.ap, op=ALU.add), [bk, xb], [xb])
    proj_fm(P, C, Wout, 0, D_MODEL, q, o_evac, st)


def conv_chunks(D):
    out = []
    for (tn, mn, ncols, key) in (("tabA", "mA", 8 * A_TAB, ("mA",)), ("tabB", "mB", 16 * 3072, ("mB",)),
                                 ("tabC0", "mC0", 32 * 4096, ("mC", 0)), ("tabC1", "mC1", 32 * 4096, ("mC", 1))):
        if tn not in D:
            continue
        c = 0
        while c < ncols:
            n = min(1024, ncols - c)
            out.append((D[tn][:, c:c + n], D[mn][:, c:c + n], n, key))
            c += n
    return out


def conv_burst(P, C, idxs):
    idxs = list(idxs)

    def abuf(idx):
        return P.sb(R_OFF + 20480 + (idx % 4) * 4096, 4096, F32)

    def load(idx):
        src, dst, n, key = C.bgq[idx]
        a = abuf(idx)
        P.dma("sp", a.ap[:, 0:n], src, [], [a])

    for k in idxs[:3]:
        load(k)
    for pos, idx in enumerate(idxs):
        if pos + 3 < len(idxs):
            load(idxs[pos + 3])
        src, dst, n, key = C.bgq[idx]
        a = abuf(idx)
        o = P.sb(R_OFF + 36864 + (idx % 2) * 2048, 2048, BF16)
        P.act(lambda e, a=a, o=o, n=n: e.activation(out=o.ap[:, 0:n], in_=a.ap[:, 0:n], func=AF.Exp), [a], [o])
        P.dma("sp", dst, o.ap[:, 0:n], [o], [Buf(None, [("d",) + key])])


def bg_step(P, C, n):
    hi = min(len(C.bgq), C.bgi + n)
    if hi > C.bgi:
        conv_burst(P, C, range(C.bgi, hi))
        C.bgi = hi


def bg_drain(P, C, keys):
    hi = C.bgi
    while hi < len(C.bgq) and C.bgq[hi][3] in keys:
        hi += 1
    if hi > C.bgi:
        conv_burst(P, C, range(C.bgi, hi))
        C.bgi = hi


def declare_dram(nc, segs, nl=DEPTH):
    D = {}

    def inp(name, shape, dt=F32):
        D[name] = nc.dram_tensor(name, list(shape), dt, kind="ExternalInput").ap()

    def internal(name, shape, dt):
        D[name] = nc.dram_tensor(name, list(shape), dt).ap()

    layers = sorted({s[1] for s in segs if s[0] in ("ffn", "mix")})
    mix_layers = sorted({s[1] for s in segs if s[0] == "mix"})
    inp("xT_in", (D_MODEL, T))
    inp("gains", (128, N_GAIN * KD))
    inp("sink", (128, 16))
    if any(s[0] == "ffn" for s in segs):
        inp("ffn_w_gate", (nl, 2, D_MODEL, D_FF))
        inp("ffn_w_up", (nl, 2, D_MODEL, D_FF))
        inp("ffn_w_down", (nl, 2, D_FF, D_MODEL))
    nab = (nl + 1) // 2
    ncl = nl // 2
    if any(l % 2 == 0 for l in mix_layers):
        inp("ab_w_in", (nab, D_MODEL, 4608))
        inp("ab_w_out", (nab, D_MODEL, D_MODEL))
        inp("tabA", (128, 8 * A_TAB))
        inp("tabB", (128, 16 * 3072))
    if any(l % 2 == 1 for l in mix_layers):
        inp("c_w_in", (max(ncl, 1), D_MODEL, 6144))
        inp("c_w_out", (max(ncl, 1), D_MODEL, D_MODEL))
        for l in mix_layers:
            if l % 2 == 1:
                inp("tabC%d" % (l // 2), (128, 32 * 4096))
    for l in mix_layers:
        nkv = MIX["ab" if l % 2 == 0 else "c"]["nkv"]
        internal("kv%d" % l, (2 * nkv * 128, T), BF16)
        internal("kvf%d" % l, (2 * 2 * nkv * 128, T), BF16)
    D["xT_out"] = nc.dram_tensor("xT_out", [D_MODEL, T], F32, kind="ExternalOutput").ap()
    return D


def build(segs, nl=DEPTH, groups=None):
    if groups is None:
        groups = [[2 * i, 2 * i + 1] for i in range(NCORES // 2)]
    nc = bass.Bass("TRN2", target_bir_lowering=False)
    D = declare_dram(nc, segs, nl)
    with ExitStack() as es:
        P = Prog(nc, es, SB_BYTES)
        C = Ctx()
        setup_common(P, C)

        def body():
            C.bgi = 0
            C.bgl = 0
            emit_consts(P, C, D)
            emit_load_x(P, C, D["xT_in"])
            for s in segs:
                if s[0] == "ffn":
                    emit_ffn(P, C, D, s[1], s[2])
                elif s[0] == "mix":
                    emit_mixer(P, C, D, s[1], groups)
                elif s[0] == "final":
                    emit_rmsnorm(P, C, DEPTH * 3, final=True)
            emit_store_x(P, C, D["xT_out"])

        P.dry = True
        body()
        P.dry = False
        body()
        block = es.enter_context(nc.Block())
        P.finalize(block)
    return nc


def _alibi(n):
    return 2.0 ** (-8.0 * np.arange(1, n + 1, dtype=np.float64) / n)


def make_tab_a(half):
    qoff = half * T
    p = np.arange(128)[:, None]
    j = np.arange(A_TAB)[None, :]
    d = j - p - 1920 + qoff
    ad = np.abs(d)
    mult = (ad <= 64).astype(np.float64) + ((d % 4 == 0) & (ad <= 256)) + ((d % 16 == 0) & (ad <= 1024))
    sl = _alibi(8)
    out = np.empty((128, 8, A_TAB), np.float32)
    with np.errstate(divide="ignore"):
        lm = np.where(mult > 0, np.log(np.maximum(mult, 1e-30)), 0.0)
    for h in range(8):
        out[:, h, :] = np.where(mult > 0, -sl[h] * ad + lm, NEG).astype(np.float32)
    return np.ascontiguousarray(out.reshape(128, 8 * A_TAB))


def make_tab_b(half):
    qoff = half * T
    sl = _alibi(8)
    out = np.empty((128, 8, 2, 6, 512), np.float32)
    p = np.arange(128)[:, None]
    qq = np.arange(512)[None, :]
    for qc in range(2):
        for blk in range(6):
            k_rel = (qc * 4 + blk) * 128 + p - 128
            q_rel = qc * 512 + qq
            d = np.abs(q_rel - k_rel)
            kg = qoff + k_rel
            valid = (d <= 128) & (kg >= 0) & (kg < SEQ)
            for h in range(8):
                out[:, h, qc, blk, :] = np.where(valid, -sl[h] * d, NEG).astype(np.float32)
    return np.ascontiguousarray(out.reshape(128, 16 * 3072))


def make_tab_c(half, rpb):
    qoff = half * T
    out = np.empty((128, 16, 2, 8, 512), np.float32)
    p = np.arange(128)[:, None]
    qq = np.arange(512)[None, :]
    for qc in range(2):
        qg = qoff + qc * 512 + qq
        r, c = qg // 64, qg % 64
        rs = np.clip(r - 4, 0, 24)
        cs = np.clip(c - 8, 0, 48)
        for blk in range(8):
            kg = qoff + (qc * 4 + blk) * 128 + p - 256
            kr, kc = kg // 64, kg % 64
            valid = (kg >= 0) & (kg < SEQ) & (kr >= rs) & (kr < rs + 8) & (kc >= cs) & (kc < cs + 16)
            ri = np.clip(kr - r + 7, 0, 14)
            ci = np.clip(kc - c + 15, 0, 30)
            g = rpb[:, ri, ci]
            out[:, :, qc, blk, :] = np.where(valid[None], g, np.float32(NEG)).transpose(1, 0, 2)
    return np.ascontiguousarray(out.reshape(128, 32 * 4096))


def fm(v):
    v = np.asarray(v, np.float32).reshape(-1, KD, 128)
    return np.ascontiguousarray(v.transpose(2, 0, 1).reshape(128, -1))


FULL_SEGS = []
for _l in range(DEPTH):
    FULL_SEGS += [("ffn", _l, 0), ("mix", _l), ("ffn", _l, 1)]
FULL_SEGS.append(("final",))


def kernel(x, ffn_norm, ffn_w_gate, ffn_w_up, ffn_w_down, mix_norm, ab_w_in, ab_w_out,
           ab_sink, c_w_in, c_w_out, c_rpb, final_norm):
    x = np.asarray(x, np.float32)
    gl = []
    for l in range(DEPTH):
        gl += [ffn_norm[l, 0], mix_norm[l], ffn_norm[l, 1]]
    gl.append(final_norm)
    gains = fm(np.stack([np.asarray(g, np.float32) for g in gl]))
    sink = np.ascontiguousarray(np.broadcast_to(np.asarray(ab_sink, np.float32).reshape(1, 16), (128, 16)))
    rpb = np.asarray(c_rpb, np.float32)
    tabs = []
    for half in range(2):
        tabs.append(dict(tabA=make_tab_a(half), tabB=make_tab_b(half),
                         tabC0=make_tab_c(half, rpb[0]), tabC1=make_tab_c(half, rpb[1])))
    shared = dict(gains=gains, sink=sink,
                  ffn_w_gate=np.asarray(ffn_w_gate, np.float32), ffn_w_up=np.asarray(ffn_w_up, np.float32),
                  ffn_w_down=np.asarray(ffn_w_down, np.float32), ab_w_in=np.asarray(ab_w_in, np.float32),
                  ab_w_out=np.asarray(ab_w_out, np.float32), c_w_in=np.asarray(c_w_in, np.float32),
                  c_w_out=np.asarray(c_w_out, np.float32))
    in_maps = []
    for c in range(NCORES):
        b, half = c // 2, c % 2
        m = dict(shared)
        m.update(tabs[half])
        m["xT_in"] = np.ascontiguousarray(x[b, half * T:(half + 1) * T, :].T)
        in_maps.append(m)
    nc = build(FULL_SEGS)
    res = run_bass_kernel_spmd(nc, in_maps, core_ids=list(range(NCORES)))
    out = np.empty((BATCH, SEQ, D_MODEL), np.float32)
    for c in range(NCORES):
        b, half = c // 2, c % 2
        out[b, half * T:(half + 1) * T, :] = res.results[c]["xT_out"].T
    return out
```

```python
import math
from contextlib import ExitStack

import numpy as np
import concourse.bass as bass
import concourse.mybir as mybir
from concourse.bass_utils import run_bass_kernel_spmd

F32 = mybir.dt.float32
BF16 = mybir.dt.bfloat16
ALU = mybir.AluOpType
AF = mybir.ActivationFunctionType

D_MODEL = 2048
BATCH = 4
SEQ = 2048
DEPTH = 4
HEAD_DIM = 128
D_FF = 5632
RMS_EPS = 1e-6
NCORES = 8
T = 1024
KD = 16
NFC = D_FF // 128
NEG = -30000.0


class Buf:
    __slots__ = ("ap", "keys")

    def __init__(self, ap, keys):
        self.ap = ap
        self.keys = keys


class Op:
    __slots__ = ("eng", "fn", "deps", "is_dma", "sem", "val", "signal", "extra_waits")

    def __init__(self, eng, fn, is_dma):
        self.eng = eng
        self.fn = fn
        self.deps = []
        self.is_dma = is_dma
        self.sem = None
        self.val = 0
        self.signal = False
        self.extra_waits = []


GRAN = 512
ENGS = ("pe", "act", "dve", "pool", "sp")
NDMASEM = 20


class Prog:
    def __init__(self, nc, es, sb_bytes):
        self.nc = nc
        self.es = es
        self.dry = False
        self.q = {e: [] for e in ENGS}
        self.last_w = {}
        self.readers = {}
        self.big = es.enter_context(nc.sbuf_tensor("big", [128, sb_bytes // 2], BF16))
        self.ps = es.enter_context(nc.psum_tensor("ps", [128, 4096], F32))
        self.eng_sem = {e: es.enter_context(nc.semaphore("s_" + e)) for e in ("pe", "act", "dve", "pool")}
        self.dma_sems = {q: [[es.enter_context(nc.semaphore("d_%s%d" % (q, i))), 0] for i in range(NDMASEM)]
                         for q in ("sp", "pool")}
        self.dma_rr = {"sp": 0, "pool": 0}
        self.cc_sem = es.enter_context(nc.semaphore("cc"))
        self.cc_cnt = 0

    def sb(self, off, nbytes, dtype=BF16, pat=None, **kw):
        assert off % 4 == 0 and nbytes % 4 == 0
        ap = self.big[:, off // 2:(off + nbytes) // 2]
        if dtype != BF16:
            ap = ap.bitcast(dtype)
        if pat is not None:
            ap = ap.rearrange(pat, **kw)
        keys = [("sb", g) for g in range(off // GRAN, (off + nbytes + GRAN - 1) // GRAN)]
        return Buf(ap, keys)

    def bank(self, b):
        return Buf(self.ps[:, b * 512:(b + 1) * 512], [("ps", b)])

    def _track(self, op, reads, writes):
        deps = {}
        lw, rd = self.last_w, self.readers
        for b in reads:
            for k in b.keys:
                w = lw.get(k)
                if w is not None:
                    deps[id(w)] = w
        for b in writes:
            for k in b.keys:
                w = lw.get(k)
                if w is not None:
                    deps[id(w)] = w
                r = rd.get(k)
                if r:
                    for o in r.values():
                        deps[id(o)] = o
        deps.pop(id(op), None)
        for b in reads:
            for k in b.keys:
                r = rd.get(k)
                if r is None:
                    r = rd[k] = {}
                r[id(op) if op.is_dma else op.eng] = op
        for b in writes:
            for k in b.keys:
                lw[k] = op
                rd[k] = {}
        for o in deps.values():
            if o.eng == "pe" and op.eng == "pe" and not o.is_dma and not op.is_dma:
                continue
            op.deps.append(o)
            o.signal = True

    def _add(self, eng, fn, reads, writes, is_dma=False):
        if self.dry:
            return None
        op = Op(eng, fn, is_dma)
        self._track(op, reads, writes)
        self.q[eng].append(op)
        return op

    def pe(self, fn, reads, writes):
        return self._add("pe", fn, reads, writes)

    def act(self, fn, reads, writes):
        return self._add("act", fn, reads, writes)

    def dve(self, fn, reads, writes):
        return self._add("dve", fn, reads, writes)

    def poolc(self, fn, reads, writes):
        return self._add("pool", fn, reads, writes)

    def dma(self, queue, out_ap, in_ap, reads, writes):
        if self.dry:
            return None
        op = self._add(queue, lambda e: e.dma_start(out=out_ap, in_=in_ap), reads, writes, is_dma=True)
        pool = self.dma_sems[queue]
        i = self.dma_rr[queue]
        self.dma_rr[queue] = (i + 1) % NDMASEM
        ent = pool[i]
        if ent[1] > 0:
            op.extra_waits.append((ent[0], ent[1]))
        ent[1] += 16
        op.sem, op.val = ent[0], ent[1]
        op.signal = True
        return op

    def collective(self, kind, in_ap, out_ap, groups, reads, writes):
        if self.dry:
            return None

        def fn(e):
            return e.collective_compute(kind, ALU.bypass, replica_groups=groups,
                                        ins=[in_ap], outs=[out_ap])
        op = self._add("pool", fn, reads, writes, is_dma=True)
        self.cc_cnt += 1
        op.sem, op.val = self.cc_sem, self.cc_cnt
        op.signal = True
        return op

    def finalize(self, block):
        for e in ("pe", "act", "dve", "pool"):
            n = 0
            for op in self.q[e]:
                if op.signal and not op.is_dma:
                    n += 1
                    op.sem, op.val = self.eng_sem[e], n
        engobj = {"pe": "tensor", "act": "scalar", "dve": "vector", "pool": "gpsimd", "sp": "sync"}

        def run(e, name):
            waited = {}
            for op in self.q[name]:
                need = {}
                for d in op.deps:
                    s = d.sem
                    if need.get(id(s), (None, 0))[1] < d.val:
                        need[id(s)] = (s, d.val)
                for (s, v) in op.extra_waits:
                    if need.get(id(s), (None, 0))[1] < v:
                        need[id(s)] = (s, v)
                for k, (s, v) in need.items():
                    if waited.get(k, 0) < v:
                        e.wait_ge(s, v)
                        waited[k] = v
                ins = op.fn(e)
                if op.signal:
                    if op.is_dma and op.sem is not self.cc_sem:
                        ins.then_inc(op.sem, 16)
                    else:
                        ins.then_inc(op.sem, 1)
            if name in self.dma_sems:
                for s, v in self.dma_sems[name]:
                    if v > 0 and waited.get(id(s), 0) < v:
                        e.wait_ge(s, v)

        for name in ENGS:
            if not self.q[name]:
                continue
            getattr(block, engobj[name])(lambda e, name=name: run(e, name))


class WStream:
    def __init__(self, P, base, nslots, slot_bytes):
        self.P = P
        self.base = base
        self.ns = nslots
        self.sbytes = slot_bytes
        self.plan = []
        self.issued = 0
        self.cursor = 0
        self.bufs = {}

    def get(self, src):
        P = self.P
        if P.dry:
            self.plan.append(src)
            return Buf(None, [])
        i = self.cursor
        self.cursor += 1
        while self.issued < min(len(self.plan), i + self.ns - 1):
            self._issue(self.issued)
            self.issued += 1
        return self.bufs.pop(i)

    def _issue(self, i):
        P = self.P
        src = self.plan[i]
        shp = list(src.shape)
        n = 1
        for s in shp[1:]:
            n *= s
        assert n * 2 <= self.sbytes, (shp, self.sbytes)
        off = self.base + (i % self.ns) * self.sbytes
        if len(shp) == 3:
            b = P.sb(off, n * 2, BF16, "p (a b) -> p a b", a=shp[1])
        else:
            b = P.sb(off, n * 2, BF16)
        P.dma("pool", b.ap, src, [], [b])
        self.bufs[i] = b


XT_OFF = 0
HT_OFF = 65536
CONST_OFF = 98304
W_OFF = 100352
W_SLOTS = 6
W_SLOT_BYTES = 8192
R_OFF = W_OFF + W_SLOTS * W_SLOT_BYTES
SB_BYTES = 206848
BFIX_OFF = 204800
R_BYTES = BFIX_OFF - R_OFF

N_GAIN = DEPTH * 3 + 1


class Ctx:
    pass


def setup_common(P, C):
    C.x = [[P.sb(XT_OFF + (kd * T + th * 512) * 4, 2048, F32) for th in range(2)] for kd in range(KD)]
    C.xk = [P.sb(XT_OFF + kd * T * 4, 4096, F32) for kd in range(KD)]
    C.h = [[P.sb(HT_OFF + (kd * T + th * 512) * 2, 1024, BF16) for th in range(2)] for kd in range(KD)]
    C.hk = [P.sb(HT_OFF + kd * T * 2, 2048, BF16) for kd in range(KD)]
    C.ones_f = P.sb(CONST_OFF, 512, F32)
    C.ones_b = P.sb(CONST_OFF + 512, 256, BF16)
    C.gains = P.sb(CONST_OFF + 768, N_GAIN * KD * 4, F32)
    C.sink = P.sb(CONST_OFF + 768 + 832, 64, F32)
    C.bfix = P.sb(BFIX_OFF, 1024, BF16)
    C.W = WStream(P, W_OFF, W_SLOTS, W_SLOT_BYTES)


def emit_consts(P, C, D):
    P.dve(lambda e: e.memset(C.ones_f.ap, 1.0 / D_MODEL), [], [C.ones_f])
    P.dve(lambda e: e.memset(C.ones_b.ap, 1.0), [], [C.ones_b])
    P.dma("sp", C.gains.ap, D["gains"], [], [C.gains])
    P.dma("sp", C.sink.ap, D["sink"], [], [C.sink])
    P.act(lambda e: e.activation(out=C.sink.ap, in_=C.sink.ap, func=AF.Exp), [C.sink], [C.sink])
    if "bfix" in D:
        P.dma("pool", C.bfix.ap[0:8, :], D["bfix"], [], [C.bfix])


def emit_load_x(P, C, src):
    v = src.rearrange("(kd p) t -> p kd t", p=128)
    for kd in range(KD):
        P.dma("sp", C.xk[kd].ap, v[:, kd, :], [], [C.xk[kd]])


def emit_store_x(P, C, dst):
    v = dst.rearrange("(kd p) t -> p kd t", p=128)
    for kd in range(KD):
        P.dma("sp", v[:, kd, :], C.xk[kd].ap, [C.xk[kd]], [])


def emit_rmsnorm(P, C, gidx, final=False):
    sqb = [P.sb(R_OFF + 36864 + i * 1024, 1024, BF16) for i in range(4)]
    rstd = [P.sb(R_OFF + 40960 + i * 2048, 2048, F32) for i in range(2)]
    for th in range(2):
        bk = P.bank(th)
        for kd in range(KD):
            xb = C.x[kd][th]
            s = sqb[(th * KD + kd) % 4]
            P.act(lambda e, xb=xb, s=s: e.activation(out=s.ap, in_=xb.ap, func=AF.Square), [xb], [s])
            P.pe(lambda e, bk=bk, s=s, kd=kd: e.matmul(bk.ap, lhsT=C.ones_b.ap, rhs=s.ap, start=(kd == 0),
                                                       stop=(kd == KD - 1)), [C.ones_b, s], [bk])
        r = rstd[th]
        P.act(lambda e, bk=bk, r=r: e.activation(out=r.ap, in_=bk.ap, func=AF.Ln, bias=RMS_EPS,
                                                 scale=1.0 / D_MODEL), [bk], [r])
        P.act(lambda e, r=r: e.activation(out=r.ap, in_=r.ap, func=AF.Exp, scale=-0.5), [r], [r])
        for kd in range(KD):
            xb = C.x[kd][th]
            g = C.gains.ap[:, gidx * KD + kd:gidx * KD + kd + 1]
            out = xb if final else C.h[kd][th]
            P.dve(lambda e, xb=xb, g=g, out=out, r=r: e.scalar_tensor_tensor(
                out=out.ap, in0=xb.ap, scalar=g, in1=r.ap, op0=ALU.mult, op1=ALU.mult), [xb, C.gains, r], [out])


def emit_ffn(P, C, D, l, j):
    emit_rmsnorm(P, C, l * 3 + (0 if j == 0 else 2))
    Wg = D["ffn_w_gate"][l, j].rearrange("(kd p) f -> p kd f", p=128)
    Wu = D["ffn_w_up"][l, j].rearrange("(kd p) f -> p kd f", p=128)
    Wd = D["ffn_w_down"][l, j].rearrange("(c p) d -> p c d", p=128)
    act = [[[P.sb(R_OFF + (s * 4 + c) * 2048 + th * 1024, 1024, BF16) for th in range(2)] for c in range(4)]
           for s in range(2)]
    sil = [P.sb(R_OFF + 16384 + i * 2048, 2048, F32) for i in range(2)]
    NG = NFC // 4
    st = {"gu": 0, "d": 0, "sil": 0}

    def GU(g):
        for hh in range(2):
            hg = 2 * g + hh
            wg = C.W.get(Wg[:, :, hg * 256:(hg + 1) * 256])
            wu = C.W.get(Wu[:, :, hg * 256:(hg + 1) * 256])
            for cc in range(2):
                c = hh * 2 + cc
                for th in range(2):
                    pb = (st["gu"] % 2) * 2
                    st["gu"] += 1
                    bG, bU = P.bank(pb), P.bank(pb + 1)
                    for (w, bk) in ((wg, bG), (wu, bU)):
                        for kd in range(KD):
                            hb = C.h[kd][th]
                            P.pe(lambda e, w=w, bk=bk, kd=kd, hb=hb, cc=cc: e.matmul(
                                bk.ap, lhsT=w.ap[:, kd, cc * 128:(cc + 1) * 128], rhs=hb.ap,
                                start=(kd == 0), stop=(kd == KD - 1)), [w, hb], [bk])
                    s = sil[st["sil"] % 2]
                    st["sil"] += 1
                    a = act[g % 2][c][th]
                    P.act(lambda e, s=s, bG=bG: e.activation(out=s.ap, in_=bG.ap, func=AF.Silu), [bG], [s])
                    P.dve(lambda e, s=s, bU=bU, a=a: e.tensor_tensor(out=a.ap, in0=s.ap, in1=bU.ap, op=ALU.mult),
                          [s, bU], [a])

    def DN(g):
        wd = [C.W.get(Wd[:, 4 * g + 2 * i:4 * g + 2 * i + 2, :]) for i in range(2)]
        for th in range(2):
            for do in range(KD):
                bk = P.bank(4 + st["d"] % 4)
                st["d"] += 1
                for c in range(4):
                    w = wd[c // 2]
                    a = act[g % 2][c][th]
                    P.pe(lambda e, w=w, bk=bk, c=c, a=a, do=do: e.matmul(
                        bk.ap, lhsT=w.ap[:, c % 2, do * 128:(do + 1) * 128], rhs=a.ap,
                        start=(c == 0), stop=(c == 3)), [w, a], [bk])
                xb = C.x[do][th]
                P.dve(lambda e, bk=bk, xb=xb: e.scalar_tensor_tensor(out=xb.ap, in0=bk.ap, scalar=0.5, in1=xb.ap,
                                                                     op0=ALU.mult, op1=ALU.add), [bk, xb], [xb])

    GU(0)
    for g in range(NG):
        if g + 1 < NG:
            GU(g + 1)
        DN(g)


QT_OFF = R_OFF
E_OFF = R_OFF + 32768
PT_OFF = R_OFF + 40960
RD_OFF = R_OFF + 45056
KST_OFF = R_OFF + 49152
VST_OFF = R_OFF + 32768
KSEQ_OFF = HT_OFF
VSEQ_OFF = HT_OFF + 8192
MSK_OFF = HT_OFF + 16384
SCALE = HEAD_DIM ** -0.5

A_TAB = 2944
DBG = {}
POOL_EVERY = 0
MIX = {
    "ab": dict(nkv=10, nq=16, ncol=4608),
    "c": dict(nkv=16, nq=16, ncol=6144),
}


def dbuf(name, *idx):
    return Buf(None, [("d", name) + tuple(idx)])


def proj_fm(P, C, Wv, col0, ncols, src, evac, st):
    for t in range(ncols // 256):
        w = C.W.get(Wv[:, :, col0 + t * 256:col0 + (t + 1) * 256])
        for cc in range(2):
            for th in range(2):
                bk = P.bank(st["b"] % 8)
                st["b"] += 1
                for kd in range(KD):
                    sb_ = src[kd][th]
                    P.pe(lambda e, w=w, bk=bk, kd=kd, sb_=sb_, cc=cc: e.matmul(
                        bk.ap, lhsT=w.ap[:, kd, cc * 128:(cc + 1) * 128], rhs=sb_.ap,
                        start=(kd == 0), stop=(kd == KD - 1)), [w, sb_], [bk])
                evac(t * 2 + cc, th, bk)


def emit_mixer(P, C, D, l, groups):
    kind = "ab" if l % 2 == 0 else "c"
    li = l // 2
    M = MIX[kind]
    nkv = M["nkv"]
    R = 2 * nkv * 128
    emit_rmsnorm(P, C, l * 3 + 1)
    Win = D["ab_w_in" if kind == "ab" else "c_w_in"][li].rearrange("(kd p) f -> p kd f", p=128)
    Wout = D["ab_w_out" if kind == "ab" else "c_w_out"][li].rearrange("(c p) d -> p c d", p=128)
    kv = D["kv%d" % l]
    kvf = D["kvf%d" % l]
    q = [[P.sb(QT_OFF + (h * T + qc * 512) * 2, 1024, BF16) for qc in range(2)] for h in range(16)]
    st = {"b": 0, "ev": 0}

    def kv_stored(blk):
        if (blk + 1) % 4 == 0 and not DBG.get("noag"):
            c = blk // 4
            rd_ = Buf(None, [("d", "kv", l, i) for i in range(4 * c, 4 * c + 4)])
            wr_ = Buf(None, [("d", "kvf", l, r, i) for r in range(2) for i in range(4 * c, 4 * c + 4)])
            P.collective("AllGather", kv[c * 512:(c + 1) * 512, :].opt(), kvf[c * 1024:(c + 1) * 1024, :].opt(),
                         groups, [rd_], [wr_])

    def copy_evac(out_buf, out_ap, bk, in_ap=None):
        in_ap = bk.ap if in_ap is None else in_ap
        st["ev"] += 1
        if st["ev"] % 2 == 0:
            P.act(lambda e: e.copy(out=out_ap, in_=in_ap), [bk], [out_buf])
        else:
            P.dve(lambda e: e.tensor_copy(out=out_ap, in_=in_ap), [bk], [out_buf])

    kst = [P.sb(KST_OFF + i * 2048, 2048, BF16) for i in range(2)]
    kcnt = {"n": 0}

    def k_proj(col0, ncols, head0):
        def evac(j, th, bk):
            ks = kst[(kcnt["n"] // 2) % 2]
            kcnt["n"] += 1
            copy_evac(ks, ks.ap[:, th * 512:(th + 1) * 512], bk)
            if th == 1:
                hd = head0 + j
                P.dma("sp", kv[hd * 128:(hd + 1) * 128, :], ks.ap, [ks], [dbuf("kv", l, hd)])
                kv_stored(hd)
        proj_fm(P, C, Win, col0, ncols, C.h, evac, st)

    vst = P.sb(VST_OFF, 8192, BF16, "p (j b d) -> p j b d", j=4, b=8)

    def v_proj(col0, ncols, head0):
        c = 0
        while c < ncols:
            gw = min(512, ncols - c)
            nh = gw // 128
            a0 = col0 + c
            if gw == 512:
                ws = [C.W.get(Win[:, 0:8, a0:a0 + 512]), C.W.get(Win[:, 8:16, a0:a0 + 512])]
                wsel = lambda kd, ws=ws: (ws[kd // 8], kd % 8)
            else:
                w = C.W.get(Win[:, :, a0:a0 + gw])
                wsel = lambda kd, w=w: (w, kd)
            for tb in range(8):
                bk = P.bank(st["b"] % 8)
                st["b"] += 1
                for kd in range(KD):
                    hb = C.hk[kd]
                    wb, wi = wsel(kd)
                    P.pe(lambda e, bk=bk, kd=kd, hb=hb, tb=tb, wb=wb, wi=wi, gw=gw: e.matmul(
                        bk.ap[:, 0:gw], lhsT=hb.ap[:, tb * 128:(tb + 1) * 128], rhs=wb.ap[:, wi, :],
                        start=(kd == 0), stop=(kd == KD - 1)), [wb, hb], [bk])
                copy_evac(vst, vst.ap[:, 0:nh, tb, :], bk, bk.ap[:, 0:gw].rearrange("p (j d) -> p j d", j=nh))
            for j in range(nh):
                hd = head0 + c // 128 + j
                r0 = (nkv + hd) * 128
                P.dma("sp", kv[r0:r0 + 128, :].rearrange("p (b d) -> p b d", b=8), vst.ap[:, j, :, :],
                      [vst], [dbuf("kv", l, nkv + hd)])
                kv_stored(nkv + hd)
            c += gw

    def q_proj(col0, ncols, head0):
        def evac(j, th, bk):
            qb = q[head0 + j][th]
            copy_evac(qb, qb.ap, bk)
        proj_fm(P, C, Win, col0, ncols, C.h, evac, st)

    if kind == "ab":
        k_proj(1024, 1024, 0)
        k_proj(4096, 256, 8)
        v_proj(2048, 1024, 0)
        v_proj(4352, 256, 8)
    else:
        k_proj(2048, 2048, 0)
        v_proj(4096, 2048, 0)
    if DBG.get("stop_after_ag"):
        return
    if kind == "ab":
        q_proj(0, 1024, 0)
        q_proj(3072, 1024, 8)
    else:
        q_proj(0, 2048, 0)

    if DBG.get("stop_after_q"):
        return
    kseq = [P.sb(KSEQ_OFF + i * 4096, 4096, BF16) for i in range(2)]
    vseq = [P.sb(VSEQ_OFF + i * 4096, 4096, BF16, "p (b d) -> p b d", d=128) for i in range(2)]
    msk = [P.sb(MSK_OFF + i * 6144, 6144, F32) for i in range(2)]
    arow_sb = P.sb(MSK_OFF + 12288, 4096, BF16)
    Tb = [P.sb(E_OFF + i * 2048, 2048, F32) for i in range(4)]
    Pt = [P.sb(PT_OFF + i * 1024, 1024, BF16) for i in range(4)]
    rd = [P.sb(RD_OFF + i * 2048, 2048, F32) for i in range(2)]
    cnt = {"kv": 0, "m": 0, "s": 0, "e": 0, "p": 0, "o": 0}

    def kvf_rows(r, i):
        r0 = ((i // 4) * 2 + r) * 512 + (i % 4) * 128
        return kvf[r0:r0 + 128, :]

    def load_kv(kvh, mode):
        s = cnt["kv"] % 2
        cnt["kv"] += 1
        ks, vs = kseq[s], vseq[s]
        if mode == "full":
            for r in range(2):
                P.dma("sp", ks.ap[:, r * 1024:(r + 1) * 1024], kvf_rows(r, kvh), [dbuf("kvf", l, r, kvh)], [ks])
                P.dma("sp", vs.ap[:, r * 8:(r + 1) * 8, :], kvf_rows(r, nkv + kvh).rearrange("p (b d) -> p b d", d=128),
                      [dbuf("kvf", l, r, nkv + kvh)], [vs])
        else:
            hb = mode
            hw = hb * 128
            P.dma("sp", ks.ap[:, 0:hw], kvf_rows(0, kvh)[:, 1024 - hw:1024], [dbuf("kvf", l, 0, kvh)], [ks])
            P.dma("sp", ks.ap[:, hw:hw + 1024], kv[kvh * 128:(kvh + 1) * 128, :], [dbuf("kv", l, kvh)], [ks])
            P.dma("sp", ks.ap[:, hw + 1024:2 * hw + 1024], kvf_rows(1, kvh)[:, 0:hw], [dbuf("kvf", l, 1, kvh)], [ks])
            vv = lambda ap: ap.rearrange("p (b d) -> p b d", d=128)
            P.dma("sp", vs.ap[:, 0:hb, :], vv(kvf_rows(0, nkv + kvh))[:, 8 - hb:8, :], [dbuf("kvf", l, 0, nkv + kvh)], [vs])
            r0 = (nkv + kvh) * 128
            P.dma("sp", vs.ap[:, hb:hb + 8, :], vv(kv[r0:r0 + 128, :]), [dbuf("kv", l, nkv + kvh)], [vs])
            P.dma("sp", vs.ap[:, hb + 8:2 * hb + 8, :], vv(kvf_rows(1, nkv + kvh))[:, 0:hb, :], [dbuf("kvf", l, 1, nkv + kvh)], [vs])
        return ks, vs

    jobs = []
    kvgroups = []

    allg = []

    def add_job(hq, qc, blocks, kvg, sink_col):
        jobs.append(dict(hq=hq, qc=qc, blocks=blocks, kvg=kvg, sink=sink_col, mt={}))
        return jobs[-1]

    if kind == "ab":
        tA, tB = D["tabA"], D["tabB"]
        for h in range(8):
            kvgroups.append((h, "full"))
            for qc in range(2):
                j = add_job(h, qc, list(range(16)), len(kvgroups) - 1, None)
                for g in range(2):
                    lo = qc * 512 - (8 * g + 7) * 128 + 1920
                    allg.append(dict(src=tA[:, h * A_TAB + lo:h * A_TAB + lo + 1408], n=1408,
                                     ent=[(j, i, (qc * 512 - i * 128 + 1920) - lo) for i in range(8 * g, 8 * g + 8)]))
        for hb in range(8):
            if hb % 4 == 0:
                kvgroups.append((8 + hb // 4, 1))
            for qc in range(2):
                c0 = (hb * 2 + qc) * 3072
                j = add_job(8 + hb, qc, [qc * 4 + i for i in range(6)], len(kvgroups) - 1, li * 8 + hb)
                for g in range(2):
                    allg.append(dict(src=tB[:, c0 + g * 1536:c0 + (g + 1) * 1536], n=1536,
                                     ent=[(j, 3 * g + k, k * 512) for k in range(3)]))
    else:
        tF = D["tabF%d" % li]
        P.dma("pool", arow_sb.ap[0:8, :], D["arow"], [], [arow_sb])
        for h in range(16):
            kvgroups.append((h, 2))
            ent = []
            for qc in range(2):
                j = add_job(h, qc, [qc * 4 + i for i in range(8)], len(kvgroups) - 1, None)
                j["rowmask"] = True
                ent += [(j, i, (14 - 2 * i) * 64) for i in range(8)]
            allg.append(dict(src=tF[:, h * C_TAB:(h + 1) * C_TAB], n=C_TAB, ent=ent))

    kvres = {}
    kvfirst = {}
    for j in jobs:
        kvfirst.setdefault(j["kvg"], id(j))
    kvfirst = {v: k for k, v in kvfirst.items()}

    def load_kvgroup(k):
        kvres[k] = load_kv(*kvgroups[k])

    tasks = [(j, i) for j in jobs for i in range(len(j["blocks"]))]
    sbank = {}
    gstart = {}
    for k, g in enumerate(allg):
        gstart[(id(g["ent"][0][0]), g["ent"][0][1])] = k

    def load_group(k):
        g = allg[k]
        m = msk[k % 2]
        P.dma("sp", m.ap[:, 0:g["n"]], g["src"], [], [m])
        for (j, i, off) in g["ent"]:
            j["mt"][i] = (m, off)

    def S(t):
        j, i = tasks[t]
        if i == 0:
            j["ks"], j["vs"] = kvres[j["kvg"]]
            par = cnt["o"] % 2
            cnt["o"] += 1
            j["bo"], j["bd"], j["r"] = P.bank(4 + par), P.bank(6), rd[par]
        bk = P.bank((0, 1, 2, 3, 7)[cnt["s"] % 5])
        cnt["s"] += 1
        sbank[t] = bk
        b = j["blocks"][i]
        ks, qb = j["ks"], q[j["hq"]][j["qc"]]
        rm = j.get("rowmask", False)
        P.pe(lambda e: e.matmul(bk.ap, lhsT=ks.ap[:, b * 128:(b + 1) * 128], rhs=qb.ap, start=True, stop=not rm),
             [ks, qb], [bk])
        if rm:
            a0 = (j["qc"] * 8 + i) * 128
            P.pe(lambda e: e.matmul(bk.ap, lhsT=arow_sb.ap[0:8, a0:a0 + 128], rhs=C.bfix.ap[0:8, :],
                                    start=False, stop=True), [arow_sb, C.bfix], [bk])

    def PV(t):
        j, i = tasks[t]
        nb = len(j["blocks"])
        k = gstart.get((id(j), i))
        if k is not None and k + 1 < len(allg):
            load_group(k + 1)
        if i == 0 and id(j) in kvfirst and kvfirst[id(j)] + 1 < len(kvgroups):
            load_kvgroup(kvfirst[id(j)] + 1)
        bk = sbank.pop(t)
        b = j["blocks"][i]
        tb = Tb[cnt["e"] % 4]
        cnt["e"] += 1
        pt = Pt[cnt["p"] % 4]
        cnt["p"] += 1
        mb, off = j["mt"][i]
        vs, bo, bd = j["vs"], j["bo"], j["bd"]
        P.dve(lambda e: e.scalar_tensor_tensor(out=tb.ap, in0=bk.ap, scalar=SCALE, in1=mb.ap[:, off:off + 512],
                                               op0=ALU.mult, op1=ALU.add), [bk, mb], [tb])
        P.act(lambda e: e.activation(out=pt.ap, in_=tb.ap, func=AF.Exp), [tb], [pt])
        P.pe(lambda e: e.matmul(bo.ap, lhsT=vs.ap[:, b, :], rhs=pt.ap, start=(i == 0), stop=(i == nb - 1)),
             [vs, pt], [bo])
        P.pe(lambda e: e.matmul(bd.ap, lhsT=C.ones_b.ap, rhs=pt.ap, start=(i == 0), stop=(i == nb - 1)),
             [C.ones_b, pt], [bd])
        if i == nb - 1:
            r, qb = j["r"], q[j["hq"]][j["qc"]]
            if j["sink"] is None:
                P.act(lambda e: e.activation(out=r.ap, in_=bd.ap, func=AF.Ln), [bd], [r])
            else:
                sc = C.sink.ap[:, j["sink"]:j["sink"] + 1]
                P.act(lambda e: e.activation(out=r.ap, in_=bd.ap, func=AF.Ln, bias=sc, scale=1.0), [bd, C.sink], [r])
            P.act(lambda e: e.activation(out=r.ap, in_=r.ap, func=AF.Exp, scale=-1.0), [r], [r])
            P.dve(lambda e: e.tensor_tensor(out=qb.ap, in0=bo.ap, in1=r.ap, op=ALU.mult), [bo, r], [qb])

    LA = 4
    load_kvgroup(0)
    load_group(0)
    for t in range(min(LA, len(tasks))):
        S(t)
    for t in range(len(tasks)):
        if t + LA < len(tasks):
            S(t + LA)
        PV(t)

    def o_evac(j, th, bk):
        xb = C.x[j][th]
        P.dve(lambda e: e.tensor_tensor(out=xb.ap, in0=bk.ap, in1=xb.ap, op=ALU.add), [bk, xb], [xb])
    proj_fm(P, C, Wout, 0, D_MODEL, q, o_evac, st)


def conv_chunks(D):
    out = []
    for (tn, mn, ncols, key) in (("tabA", "mA", 8 * A_TAB, ("mA",)), ("tabB", "mB", 16 * 3072, ("mB",)),
                                 ("tabC0", "mC0", 32 * 4096, ("mC", 0)), ("tabC1", "mC1", 32 * 4096, ("mC", 1))):
        if tn not in D:
            continue
        c = 0
        while c < ncols:
            n = min(1024, ncols - c)
            out.append((D[tn][:, c:c + n], D[mn][:, c:c + n], n, key))
            c += n
    return out


def conv_burst(P, C, idxs):
    idxs = list(idxs)

    def abuf(idx):
        return P.sb(R_OFF + 20480 + (idx % 4) * 4096, 4096, F32)

    def load(idx):
        src, dst, n, key = C.bgq[idx]
        a = abuf(idx)
        P.dma("sp", a.ap[:, 0:n], src, [], [a])

    for k in idxs[:3]:
        load(k)
    for pos, idx in enumerate(idxs):
        if pos + 3 < len(idxs):
            load(idxs[pos + 3])
        src, dst, n, key = C.bgq[idx]
        a = abuf(idx)
        o = P.sb(R_OFF + 36864 + (idx % 2) * 2048, 2048, BF16)
        P.act(lambda e, a=a, o=o, n=n: e.activation(out=o.ap[:, 0:n], in_=a.ap[:, 0:n], func=AF.Exp), [a], [o])
        P.dma("sp", dst, o.ap[:, 0:n], [o], [Buf(None, [("d",) + key])])


def bg_step(P, C, n):
    hi = min(len(C.bgq), C.bgi + n)
    if hi > C.bgi:
        conv_burst(P, C, range(C.bgi, hi))
        C.bgi = hi


def bg_drain(P, C, keys):
    hi = C.bgi
    while hi < len(C.bgq) and C.bgq[hi][3] in keys:
        hi += 1
    if hi > C.bgi:
        conv_burst(P, C, range(C.bgi, hi))
        C.bgi = hi


def declare_dram(nc, segs, nl=DEPTH):
    D = {}

    def inp(name, shape, dt=F32):
        D[name] = nc.dram_tensor(name, list(shape), dt, kind="ExternalInput").ap()

    def internal(name, shape, dt):
        D[name] = nc.dram_tensor(name, list(shape), dt).ap()

    layers = sorted({s[1] for s in segs if s[0] in ("ffn", "mix")})
    mix_layers = sorted({s[1] for s in segs if s[0] == "mix"})
    inp("xT_in", (D_MODEL, T))
    inp("gains", (128, N_GAIN * KD))
    inp("sink", (128, 16))
    if any(s[0] == "ffn" for s in segs):
        inp("ffn_w_gate", (nl, 2, D_MODEL, D_FF))
        inp("ffn_w_up", (nl, 2, D_MODEL, D_FF))
        inp("ffn_w_down", (nl, 2, D_FF, D_MODEL))
    nab = (nl + 1) // 2
    ncl = nl // 2
    if any(l % 2 == 0 for l in mix_layers):
        inp("ab_w_in", (nab, D_MODEL, 4608))
        inp("ab_w_out", (nab, D_MODEL, D_MODEL))
        inp("tabA", (128, 8 * A_TAB))
        inp("tabB", (128, 16 * 3072))
    if any(l % 2 == 1 for l in mix_layers):
        inp("c_w_in", (max(ncl, 1), D_MODEL, 6144))
        inp("c_w_out", (max(ncl, 1), D_MODEL, D_MODEL))
        for l in mix_layers:
            if l % 2 == 1:
                inp("tabF%d" % (l // 2), (128, 16 * C_TAB))
        inp("arow", (8, 2048))
        inp("bfix", (8, 512))
    for l in mix_layers:
        nkv = MIX["ab" if l % 2 == 0 else "c"]["nkv"]
        internal("kv%d" % l, (2 * nkv * 128, T), BF16)
        internal("kvf%d" % l, (2 * 2 * nkv * 128, T), BF16)
    D["xT_out"] = nc.dram_tensor("xT_out", [D_MODEL, T], F32, kind="ExternalOutput").ap()
    return D


def build(segs, nl=DEPTH, groups=None):
    if groups is None:
        groups = [[2 * i, 2 * i + 1] for i in range(NCORES // 2)]
    nc = bass.Bass("TRN2", target_bir_lowering=False)
    D = declare_dram(nc, segs, nl)
    with ExitStack() as es:
        P = Prog(nc, es, SB_BYTES)
        C = Ctx()
        setup_common(P, C)

        def body():
            C.bgi = 0
            C.bgl = 0
            emit_consts(P, C, D)
            emit_load_x(P, C, D["xT_in"])
            for s in segs:
                if s[0] == "ffn":
                    emit_ffn(P, C, D, s[1], s[2])
                elif s[0] == "mix":
                    emit_mixer(P, C, D, s[1], groups)
                elif s[0] == "final":
                    emit_rmsnorm(P, C, DEPTH * 3, final=True)
            emit_store_x(P, C, D["xT_out"])

        P.dry = True
        body()
        P.dry = False
        body()
        block = es.enter_context(nc.Block())
        P.finalize(block)
    return nc


def _alibi(n):
    return 2.0 ** (-8.0 * np.arange(1, n + 1, dtype=np.float64) / n)


def make_tab_a(half):
    qoff = half * T
    p = np.arange(128)[:, None]
    j = np.arange(A_TAB)[None, :]
    d = j - p - 1920 + qoff
    ad = np.abs(d)
    mult = (ad <= 64).astype(np.float64) + ((d % 4 == 0) & (ad <= 256)) + ((d % 16 == 0) & (ad <= 1024))
    sl = _alibi(8)
    out = np.empty((128, 8, A_TAB), np.float32)
    with np.errstate(divide="ignore"):
        lm = np.where(mult > 0, np.log(np.maximum(mult, 1e-30)), 0.0)
    for h in range(8):
        out[:, h, :] = np.where(mult > 0, -sl[h] * ad + lm, NEG).astype(np.float32)
    return np.ascontiguousarray(out.reshape(128, 8 * A_TAB))


def make_tab_b(half):
    qoff = half * T
    sl = _alibi(8)
    out = np.empty((128, 8, 2, 6, 512), np.float32)
    p = np.arange(128)[:, None]
    qq = np.arange(512)[None, :]
    for qc in range(2):
        for blk in range(6):
            k_rel = (qc * 4 + blk) * 128 + p - 128
            q_rel = qc * 512 + qq
            d = np.abs(q_rel - k_rel)
            kg = qoff + k_rel
            valid = (d <= 128) & (kg >= 0) & (kg < SEQ)
            for h in range(8):
                out[:, h, qc, blk, :] = np.where(valid, -sl[h] * d, NEG).astype(np.float32)
    return np.ascontiguousarray(out.reshape(128, 16 * 3072))


def make_tab_c(half, rpb):
    qoff = half * T
    out = np.empty((128, 16, 2, 8, 512), np.float32)
    p = np.arange(128)[:, None]
    qq = np.arange(512)[None, :]
    for qc in range(2):
        qg = qoff + qc * 512 + qq
        r, c = qg // 64, qg % 64
        rs = np.clip(r - 4, 0, 24)
        cs = np.clip(c - 8, 0, 48)
        for blk in range(8):
            kg = qoff + (qc * 4 + blk) * 128 + p - 256
            kr, kc = kg // 64, kg % 64
            valid = (kg >= 0) & (kg < SEQ) & (kr >= rs) & (kr < rs + 8) & (kc >= cs) & (kc < cs + 16)
            ri = np.clip(kr - r + 7, 0, 14)
            ci = np.clip(kc - c + 15, 0, 30)
            g = rpb[:, ri, ci]
            out[:, :, qc, blk, :] = np.where(valid[None], g, np.float32(NEG)).transpose(1, 0, 2)
    return np.ascontiguousarray(out.reshape(128, 32 * 4096))


C_TAB = 22 * 64


def make_tab_c2(half, rpb):
    qoff = half * T
    p = np.arange(128)
    hf, kc = p // 64, p % 64
    u = np.arange(22)
    c = np.arange(64)
    dr = hf[:, None] + 10 - u[None, :]
    cs = np.clip(c - 8, 0, 48)
    colv = (kc[:, None] >= cs[None, :]) & (kc[:, None] < cs[None, :] + 16)
    dc = kc[:, None] - c[None, :]
    ri = np.clip(dr + 7, 0, 14)
    ci = np.clip(dc + 15, 0, 30)
    valid = (np.abs(dr) <= 7)[:, :, None] & colv[:, None, :]
    g = rpb[:, ri[:, :, None], ci[:, None, :]]
    F = np.where(valid[None], g, np.float32(NEG)).astype(np.float32).transpose(1, 0, 2, 3)
    arow = np.empty((8, 2, 8, 128), np.float32)
    for qc in range(2):
        for blk in range(8):
            kg0 = qoff + (qc * 4 + blk) * 128 - 256
            kr = kg0 // 64 + hf
            r = (qoff + qc * 512) // 64 + np.arange(8)
            rs = np.clip(r - 4, 0, 24)
            ok = (kr[None, :] >= 0) & (kr[None, :] < 32) & (kr[None, :] >= rs[:, None]) & (kr[None, :] < rs[:, None] + 8)
            arow[:, qc, blk, :] = np.where(ok, 0.0, NEG)
    return np.ascontiguousarray(F.reshape(128, 16 * C_TAB)), np.ascontiguousarray(arow.reshape(8, 2048))


def make_bfix():
    b = np.zeros((8, 8, 64), np.float32)
    for r in range(8):
        b[r, r, :] = 1.0
    return b.reshape(8, 512)


def fm(v):
    v = np.asarray(v, np.float32).reshape(-1, KD, 128)
    return np.ascontiguousarray(v.transpose(2, 0, 1).reshape(128, -1))


FULL_SEGS = []
for _l in range(DEPTH):
    FULL_SEGS += [("ffn", _l, 0), ("mix", _l), ("ffn", _l, 1)]
FULL_SEGS.append(("final",))


def kernel(x, ffn_norm, ffn_w_gate, ffn_w_up, ffn_w_down, mix_norm, ab_w_in, ab_w_out,
           ab_sink, c_w_in, c_w_out, c_rpb, final_norm):
    x = np.asarray(x, np.float32)
    gl = []
    for l in range(DEPTH):
        gl += [ffn_norm[l, 0], mix_norm[l], ffn_norm[l, 1]]
    gl.append(final_norm)
    gains = fm(np.stack([np.asarray(g, np.float32) for g in gl]))
    sink = np.ascontiguousarray(np.broadcast_to(np.asarray(ab_sink, np.float32).reshape(1, 16), (128, 16)))
    rpb = np.asarray(c_rpb, np.float32)
    tabs = []
    for half in range(2):
        f0, ar = make_tab_c2(half, rpb[0])
        f1, _ = make_tab_c2(half, rpb[1])
        tabs.append(dict(tabA=make_tab_a(half), tabB=make_tab_b(half), tabF0=f0, tabF1=f1, arow=ar,
                         bfix=make_bfix()))
    shared = dict(gains=gains, sink=sink,
                  ffn_w_gate=np.asarray(ffn_w_gate, np.float32), ffn_w_up=np.asarray(ffn_w_up, np.float32),
                  ffn_w_down=np.asarray(ffn_w_down, np.float32), ab_w_in=np.asarray(ab_w_in, np.float32),
                  ab_w_out=np.asarray(ab_w_out, np.float32), c_w_in=np.asarray(c_w_in, np.float32),
                  c_w_out=np.asarray(c_w_out, np.float32))
    in_maps = []
    for c in range(NCORES):
        b, half = c // 2, c % 2
        m = dict(shared)
        m.update(tabs[half])
        m["xT_in"] = np.ascontiguousarray(x[b, half * T:(half + 1) * T, :].T)
        in_maps.append(m)
    nc = build(FULL_SEGS)
    res = run_bass_kernel_spmd(nc, in_maps, core_ids=list(range(NCORES)))
    out = np.empty((BATCH, SEQ, D_MODEL), np.float32)
    for c in range(NCORES):
        b, half = c // 2, c % 2
        out[b, half * T:(half + 1) * T, :] = res.results[c]["xT_out"].T
    return out
```

```python
import math
from contextlib import ExitStack

import numpy as np
import concourse.bass as bass
import concourse.mybir as mybir
from concourse.bass_utils import run_bass_kernel_spmd

F32 = mybir.dt.float32
BF16 = mybir.dt.bfloat16
ALU = mybir.AluOpType
AF = mybir.ActivationFunctionType

D_MODEL = 2048
BATCH = 4
SEQ = 2048
DEPTH = 4
HEAD_DIM = 128
D_FF = 5632
RMS_EPS = 1e-6
NCORES = 8
T = 1024
KD = 16
NFC = D_FF // 128
NEG = -30000.0


class Buf:
    __slots__ = ("ap", "keys")

    def __init__(self, ap, keys):
        self.ap = ap
        self.keys = keys


class Op:
    __slots__ = ("eng", "fn", "deps", "is_dma", "sem", "val", "signal", "extra_waits")

    def __init__(self, eng, fn, is_dma):
        self.eng = eng
        self.fn = fn
        self.deps = []
        self.is_dma = is_dma
        self.sem = None
        self.val = 0
        self.signal = False
        self.extra_waits = []


GRAN = 512
ENGS = ("pe", "act", "dve", "pool", "sp")
NDMASEM = 20


class Prog:
    def __init__(self, nc, es, sb_bytes):
        self.nc = nc
        self.es = es
        self.dry = False
        self.q = {e: [] for e in ENGS}
        self.last_w = {}
        self.readers = {}
        self.big = es.enter_context(nc.sbuf_tensor("big", [128, sb_bytes // 2], BF16))
        self.ps = es.enter_context(nc.psum_tensor("ps", [128, 4096], F32))
        self.eng_sem = {e: es.enter_context(nc.semaphore("s_" + e)) for e in ("pe", "act", "dve", "pool")}
        self.dma_sems = {q: [[es.enter_context(nc.semaphore("d_%s%d" % (q, i))), 0] for i in range(NDMASEM)]
                         for q in ("sp", "pool")}
        self.dma_rr = {"sp": 0, "pool": 0}
        self.cc_sem = es.enter_context(nc.semaphore("cc"))
        self.cc_cnt = 0

    def sb(self, off, nbytes, dtype=BF16, pat=None, **kw):
        assert off % 4 == 0 and nbytes % 4 == 0
        ap = self.big[:, off // 2:(off + nbytes) // 2]
        if dtype != BF16:
            ap = ap.bitcast(dtype)
        if pat is not None:
            ap = ap.rearrange(pat, **kw)
        keys = [("sb", g) for g in range(off // GRAN, (off + nbytes + GRAN - 1) // GRAN)]
        return Buf(ap, keys)

    def bank(self, b):
        return Buf(self.ps[:, b * 512:(b + 1) * 512], [("ps", b)])

    def _track(self, op, reads, writes):
        deps = {}
        lw, rd = self.last_w, self.readers
        for b in reads:
            for k in b.keys:
                w = lw.get(k)
                if w is not None:
                    deps[id(w)] = w
        for b in writes:
            for k in b.keys:
                w = lw.get(k)
                if w is not None:
                    deps[id(w)] = w
                r = rd.get(k)
                if r:
                    for o in r.values():
                        deps[id(o)] = o
        deps.pop(id(op), None)
        for b in reads:
            for k in b.keys:
                r = rd.get(k)
                if r is None:
                    r = rd[k] = {}
                r[id(op) if op.is_dma else op.eng] = op
        for b in writes:
            for k in b.keys:
                lw[k] = op
                rd[k] = {}
        for o in deps.values():
            if o.eng == "pe" and op.eng == "pe" and not o.is_dma and not op.is_dma:
                continue
            op.deps.append(o)
            o.signal = True

    def _add(self, eng, fn, reads, writes, is_dma=False):
        if self.dry:
            return None
        op = Op(eng, fn, is_dma)
        self._track(op, reads, writes)
        self.q[eng].append(op)
        return op

    def pe(self, fn, reads, writes):
        return self._add("pe", fn, reads, writes)

    def act(self, fn, reads, writes):
        return self._add("act", fn, reads, writes)

    def dve(self, fn, reads, writes):
        return self._add("dve", fn, reads, writes)

    def poolc(self, fn, reads, writes):
        return self._add("pool", fn, reads, writes)

    def dma(self, queue, out_ap, in_ap, reads, writes):
        if self.dry:
            return None
        op = self._add(queue, lambda e: e.dma_start(out=out_ap, in_=in_ap), reads, writes, is_dma=True)
        pool = self.dma_sems[queue]
        i = self.dma_rr[queue]
        self.dma_rr[queue] = (i + 1) % NDMASEM
        ent = pool[i]
        if ent[1] > 0:
            op.extra_waits.append((ent[0], ent[1]))
        ent[1] += 16
        op.sem, op.val = ent[0], ent[1]
        op.signal = True
        return op

    def collective(self, kind, in_ap, out_ap, groups, reads, writes):
        if self.dry:
            return None

        def fn(e):
            return e.collective_compute(kind, ALU.bypass, replica_groups=groups,
                                        ins=[in_ap], outs=[out_ap])
        op = self._add("pool", fn, reads, writes, is_dma=True)
        self.cc_cnt += 1
        op.sem, op.val = self.cc_sem, self.cc_cnt
        op.signal = True
        return op

    def finalize(self, block):
        for e in ("pe", "act", "dve", "pool"):
            n = 0
            for op in self.q[e]:
                if op.signal and not op.is_dma:
                    n += 1
                    op.sem, op.val = self.eng_sem[e], n
        engobj = {"pe": "tensor", "act": "scalar", "dve": "vector", "pool": "gpsimd", "sp": "sync"}

        def run(e, name):
            waited = {}
            for op in self.q[name]:
                need = {}
                for d in op.deps:
                    s = d.sem
                    if need.get(id(s), (None, 0))[1] < d.val:
                        need[id(s)] = (s, d.val)
                for (s, v) in op.extra_waits:
                    if need.get(id(s), (None, 0))[1] < v:
                        need[id(s)] = (s, v)
                for k, (s, v) in need.items():
                    if waited.get(k, 0) < v:
                        e.wait_ge(s, v)
                        waited[k] = v
                ins = op.fn(e)
                if op.signal:
                    if op.is_dma and op.sem is not self.cc_sem:
                        ins.then_inc(op.sem, 16)
                    else:
                        ins.then_inc(op.sem, 1)
            if name in self.dma_sems:
                for s, v in self.dma_sems[name]:
                    if v > 0 and waited.get(id(s), 0) < v:
                        e.wait_ge(s, v)

        for name in ENGS:
            if not self.q[name]:
                continue
            getattr(block, engobj[name])(lambda e, name=name: run(e, name))


class WStream:
    def __init__(self, P, base, nslots, slot_bytes):
        self.P = P
        self.base = base
        self.ns = nslots
        self.sbytes = slot_bytes
        self.plan = []
        self.issued = 0
        self.cursor = 0
        self.bufs = {}

    def get(self, src):
        P = self.P
        if P.dry:
            self.plan.append(src)
            return Buf(None, [])
        i = self.cursor
        self.cursor += 1
        while self.issued < min(len(self.plan), i + self.ns - 1):
            self._issue(self.issued)
            self.issued += 1
        return self.bufs.pop(i)

    def _issue(self, i):
        P = self.P
        src = self.plan[i]
        shp = list(src.shape)
        n = 1
        for s in shp[1:]:
            n *= s
        assert n * 2 <= self.sbytes, (shp, self.sbytes)
        off = self.base + (i % self.ns) * self.sbytes
        if len(shp) == 3:
            b = P.sb(off, n * 2, BF16, "p (a b) -> p a b", a=shp[1])
        else:
            b = P.sb(off, n * 2, BF16)
        P.dma("pool", b.ap, src, [], [b])
        self.bufs[i] = b


XT_OFF = 0
HT_OFF = 65536
CONST_OFF = 98304
W_OFF = 100352
W_SLOTS = 6
W_SLOT_BYTES = 8192
R_OFF = W_OFF + W_SLOTS * W_SLOT_BYTES
SB_BYTES = 206848
BFIX_OFF = 204800
R_BYTES = BFIX_OFF - R_OFF

N_GAIN = DEPTH * 3 + 1


class Ctx:
    pass


def setup_common(P, C):
    C.x = [[P.sb(XT_OFF + (kd * T + th * 512) * 4, 2048, F32) for th in range(2)] for kd in range(KD)]
    C.xk = [P.sb(XT_OFF + kd * T * 4, 4096, F32) for kd in range(KD)]
    C.h = [[P.sb(HT_OFF + (kd * T + th * 512) * 2, 1024, BF16) for th in range(2)] for kd in range(KD)]
    C.hk = [P.sb(HT_OFF + kd * T * 2, 2048, BF16) for kd in range(KD)]
    C.ones_f = P.sb(CONST_OFF, 512, F32)
    C.ones_b = P.sb(CONST_OFF + 512, 256, BF16)
    C.gains = P.sb(CONST_OFF + 768, N_GAIN * KD * 4, F32)
    C.sink = P.sb(CONST_OFF + 768 + 832, 64, F32)
    C.bfix = P.sb(BFIX_OFF, 1024, BF16)
    C.W = WStream(P, W_OFF, W_SLOTS, W_SLOT_BYTES)


def emit_consts(P, C, D):
    P.dve(lambda e: e.memset(C.ones_f.ap, 1.0 / D_MODEL), [], [C.ones_f])
    P.dve(lambda e: e.memset(C.ones_b.ap, 1.0), [], [C.ones_b])
    P.dma("sp", C.gains.ap, D["gains"], [], [C.gains])
    P.dma("sp", C.sink.ap, D["sink"], [], [C.sink])
    P.act(lambda e: e.activation(out=C.sink.ap, in_=C.sink.ap, func=AF.Exp), [C.sink], [C.sink])
    if "bfix" in D:
        P.dma("pool", C.bfix.ap[0:8, :], D["bfix"], [], [C.bfix])


def emit_load_x(P, C, src):
    v = src.rearrange("(kd p) t -> p kd t", p=128)
    for th in range(2):
        for kd in range(KD):
            P.dma("sp", C.x[kd][th].ap, v[:, kd, th * 512:(th + 1) * 512], [], [C.x[kd][th]])


def emit_store_x(P, C, dst):
    v = dst.rearrange("(kd p) t -> p kd t", p=128)
    for kd in range(KD):
        P.dma("sp", v[:, kd, :], C.xk[kd].ap, [C.xk[kd]], [])


def emit_rmsnorm(P, C, gidx, final=False):
    sqb = [P.sb(R_OFF + 36864 + i * 1024, 1024, BF16) for i in range(4)]
    rstd = [P.sb(R_OFF + 40960 + i * 2048, 2048, F32) for i in range(2)]
    for th in range(2):
        bk = P.bank(th)
        for kd in range(KD):
            xb = C.x[kd][th]
            s = sqb[(th * KD + kd) % 4]
            P.act(lambda e, xb=xb, s=s: e.activation(out=s.ap, in_=xb.ap, func=AF.Square), [xb], [s])
            P.pe(lambda e, bk=bk, s=s, kd=kd: e.matmul(bk.ap, lhsT=C.ones_b.ap, rhs=s.ap, start=(kd == 0),
                                                       stop=(kd == KD - 1)), [C.ones_b, s], [bk])
        r = rstd[th]
        P.act(lambda e, bk=bk, r=r: e.activation(out=r.ap, in_=bk.ap, func=AF.Ln, bias=RMS_EPS,
                                                 scale=1.0 / D_MODEL), [bk], [r])
        P.act(lambda e, r=r: e.activation(out=r.ap, in_=r.ap, func=AF.Exp, scale=-0.5), [r], [r])
        for kd in range(KD):
            xb = C.x[kd][th]
            g = C.gains.ap[:, gidx * KD + kd:gidx * KD + kd + 1]
            out = xb if final else C.h[kd][th]
            P.dve(lambda e, xb=xb, g=g, out=out, r=r: e.scalar_tensor_tensor(
                out=out.ap, in0=xb.ap, scalar=g, in1=r.ap, op0=ALU.mult, op1=ALU.mult), [xb, C.gains, r], [out])


def emit_ffn(P, C, D, l, j):
    emit_rmsnorm(P, C, l * 3 + (0 if j == 0 else 2))
    Wg = D["ffn_w_gate"][l, j].rearrange("(kd p) f -> p kd f", p=128)
    Wu = D["ffn_w_up"][l, j].rearrange("(kd p) f -> p kd f", p=128)
    Wd = D["ffn_w_down"][l, j].rearrange("(c p) d -> p c d", p=128)
    act = [[[P.sb(R_OFF + (s * 4 + c) * 2048 + th * 1024, 1024, BF16) for th in range(2)] for c in range(4)]
           for s in range(2)]
    sil = [P.sb(R_OFF + 16384 + i * 2048, 2048, F32) for i in range(2)]
    NG = NFC // 4
    st = {"gu": 0, "d": 0, "sil": 0}

    def GU(g):
        for hh in range(2):
            hg = 2 * g + hh
            wg = C.W.get(Wg[:, :, hg * 256:(hg + 1) * 256])
            wu = C.W.get(Wu[:, :, hg * 256:(hg + 1) * 256])
            for cc in range(2):
                c = hh * 2 + cc
                for th in range(2):
                    pb = (st["gu"] % 2) * 2
                    st["gu"] += 1
                    bG, bU = P.bank(pb), P.bank(pb + 1)
                    for (w, bk) in ((wg, bG), (wu, bU)):
                        for kd in range(KD):
                            hb = C.h[kd][th]
                            P.pe(lambda e, w=w, bk=bk, kd=kd, hb=hb, cc=cc: e.matmul(
                                bk.ap, lhsT=w.ap[:, kd, cc * 128:(cc + 1) * 128], rhs=hb.ap,
                                start=(kd == 0), stop=(kd == KD - 1)), [w, hb], [bk])
                    s = sil[st["sil"] % 2]
                    st["sil"] += 1
                    a = act[g % 2][c][th]
                    P.act(lambda e, s=s, bG=bG: e.activation(out=s.ap, in_=bG.ap, func=AF.Silu), [bG], [s])
                    P.dve(lambda e, s=s, bU=bU, a=a: e.tensor_tensor(out=a.ap, in0=s.ap, in1=bU.ap, op=ALU.mult),
                          [s, bU], [a])

    def DN(g):
        wd = [C.W.get(Wd[:, 4 * g + 2 * i:4 * g + 2 * i + 2, :]) for i in range(2)]
        for th in range(2):
            for do in range(KD):
                bk = P.bank(4 + st["d"] % 4)
                st["d"] += 1
                for c in range(4):
                    w = wd[c // 2]
                    a = act[g % 2][c][th]
                    P.pe(lambda e, w=w, bk=bk, c=c, a=a, do=do: e.matmul(
                        bk.ap, lhsT=w.ap[:, c % 2, do * 128:(do + 1) * 128], rhs=a.ap,
                        start=(c == 0), stop=(c == 3)), [w, a], [bk])
                xb = C.x[do][th]
                P.dve(lambda e, bk=bk, xb=xb: e.scalar_tensor_tensor(out=xb.ap, in0=bk.ap, scalar=0.5, in1=xb.ap,
                                                                     op0=ALU.mult, op1=ALU.add), [bk, xb], [xb])

    GU(0)
    for g in range(NG):
        if g + 1 < NG:
            GU(g + 1)
        DN(g)


QT_OFF = R_OFF
E_OFF = R_OFF + 32768
PT_OFF = R_OFF + 40960
RD_OFF = R_OFF + 45056
KST_OFF = R_OFF + 49152
VST_OFF = R_OFF + 32768
KSEQ_OFF = HT_OFF
VSEQ_OFF = HT_OFF + 8192
MSK_OFF = HT_OFF + 16384
SCALE = HEAD_DIM ** -0.5

A_TAB = 2944
DBG = {}
POOL_EVERY = 0
MIX = {
    "ab": dict(nkv=10, nq=16, ncol=4608),
    "c": dict(nkv=16, nq=16, ncol=6144),
}


def dbuf(name, *idx):
    return Buf(None, [("d", name) + tuple(idx)])


def proj_fm(P, C, Wv, col0, ncols, src, evac, st):
    for t in range(ncols // 256):
        w = C.W.get(Wv[:, :, col0 + t * 256:col0 + (t + 1) * 256])
        for cc in range(2):
            for th in range(2):
                bk = P.bank(st["b"] % 8)
                st["b"] += 1
                for kd in range(KD):
                    sb_ = src[kd][th]
                    P.pe(lambda e, w=w, bk=bk, kd=kd, sb_=sb_, cc=cc: e.matmul(
                        bk.ap, lhsT=w.ap[:, kd, cc * 128:(cc + 1) * 128], rhs=sb_.ap,
                        start=(kd == 0), stop=(kd == KD - 1)), [w, sb_], [bk])
                evac(t * 2 + cc, th, bk)


def emit_mixer(P, C, D, l, groups):
    kind = "ab" if l % 2 == 0 else "c"
    li = l // 2
    M = MIX[kind]
    nkv = M["nkv"]
    R = 2 * nkv * 128
    emit_rmsnorm(P, C, l * 3 + 1)
    Win = D["ab_w_in" if kind == "ab" else "c_w_in"][li].rearrange("(kd p) f -> p kd f", p=128)
    Wout = D["ab_w_out" if kind == "ab" else "c_w_out"][li].rearrange("(c p) d -> p c d", p=128)
    kv = D["kv%d" % l]
    kvf = D["kvf%d" % l]
    q = [[P.sb(QT_OFF + (h * T + qc * 512) * 2, 1024, BF16) for qc in range(2)] for h in range(16)]
    st = {"b": 0, "ev": 0}

    def kv_stored(blk):
        if (blk + 1) % 4 == 0 and not DBG.get("noag"):
            c = blk // 4
            rd_ = Buf(None, [("d", "kv", l, i) for i in range(4 * c, 4 * c + 4)])
            wr_ = Buf(None, [("d", "kvf", l, r, i) for r in range(2) for i in range(4 * c, 4 * c + 4)])
            P.collective("AllGather", kv[c * 512:(c + 1) * 512, :].opt(), kvf[c * 1024:(c + 1) * 1024, :].opt(),
                         groups, [rd_], [wr_])

    def copy_evac(out_buf, out_ap, bk, in_ap=None):
        in_ap = bk.ap if in_ap is None else in_ap
        st["ev"] += 1
        if st["ev"] % 2 == 0:
            P.act(lambda e: e.copy(out=out_ap, in_=in_ap), [bk], [out_buf])
        else:
            P.dve(lambda e: e.tensor_copy(out=out_ap, in_=in_ap), [bk], [out_buf])

    kst = [P.sb(KST_OFF + i * 2048, 2048, BF16) for i in range(2)]
    kcnt = {"n": 0}

    def k_proj(col0, ncols, head0):
        def evac(j, th, bk):
            ks = kst[(kcnt["n"] // 2) % 2]
            kcnt["n"] += 1
            copy_evac(ks, ks.ap[:, th * 512:(th + 1) * 512], bk)
            if th == 1:
                hd = head0 + j
                P.dma("sp", kv[hd * 128:(hd + 1) * 128, :], ks.ap, [ks], [dbuf("kv", l, hd)])
                kv_stored(hd)
        proj_fm(P, C, Win, col0, ncols, C.h, evac, st)

    vst = P.sb(VST_OFF, 8192, BF16, "p (j b d) -> p j b d", j=4, b=8)

    def v_proj(col0, ncols, head0):
        c = 0
        while c < ncols:
            gw = min(512, ncols - c)
            nh = gw // 128
            a0 = col0 + c
            if gw == 512:
                ws = [C.W.get(Win[:, 0:8, a0:a0 + 512]), C.W.get(Win[:, 8:16, a0:a0 + 512])]
                wsel = lambda kd, ws=ws: (ws[kd // 8], kd % 8)
            else:
                w = C.W.get(Win[:, :, a0:a0 + gw])
                wsel = lambda kd, w=w: (w, kd)
            for tb in range(8):
                bk = P.bank(st["b"] % 8)
                st["b"] += 1
                for kd in range(KD):
                    hb = C.hk[kd]
                    wb, wi = wsel(kd)
                    P.pe(lambda e, bk=bk, kd=kd, hb=hb, tb=tb, wb=wb, wi=wi, gw=gw: e.matmul(
                        bk.ap[:, 0:gw], lhsT=hb.ap[:, tb * 128:(tb + 1) * 128], rhs=wb.ap[:, wi, :],
                        start=(kd == 0), stop=(kd == KD - 1)), [wb, hb], [bk])
                copy_evac(vst, vst.ap[:, 0:nh, tb, :], bk, bk.ap[:, 0:gw].rearrange("p (j d) -> p j d", j=nh))
            for j in range(nh):
                hd = head0 + c // 128 + j
                r0 = (nkv + hd) * 128
                P.dma("sp", kv[r0:r0 + 128, :].rearrange("p (b d) -> p b d", b=8), vst.ap[:, j, :, :],
                      [vst], [dbuf("kv", l, nkv + hd)])
                kv_stored(nkv + hd)
            c += gw

    def q_proj(col0, ncols, head0):
        def evac(j, th, bk):
            qb = q[head0 + j][th]
            copy_evac(qb, qb.ap, bk)
        proj_fm(P, C, Win, col0, ncols, C.h, evac, st)

    if kind == "ab":
        k_proj(1024, 1024, 0)
        k_proj(4096, 256, 8)
        v_proj(2048, 1024, 0)
        v_proj(4352, 256, 8)
    else:
        k_proj(2048, 2048, 0)
        v_proj(4096, 2048, 0)
    if DBG.get("stop_after_ag"):
        return
    if kind == "ab":
        q_proj(0, 1024, 0)
        q_proj(3072, 1024, 8)
    else:
        q_proj(0, 2048, 0)

    if DBG.get("stop_after_q"):
        return
    kseq = [P.sb(KSEQ_OFF + i * 4096, 4096, BF16) for i in range(2)]
    vseq = [P.sb(VSEQ_OFF + i * 4096, 4096, BF16, "p (b d) -> p b d", d=128) for i in range(2)]
    msk = [P.sb(MSK_OFF + i * 6144, 6144, F32) for i in range(2)]
    arow_sb = P.sb(MSK_OFF + 12288, 4096, BF16)
    Tb = [P.sb(E_OFF + i * 2048, 2048, F32) for i in range(4)]
    Pt = [P.sb(PT_OFF + i * 1024, 1024, BF16) for i in range(4)]
    rd = [P.sb(RD_OFF + i * 2048, 2048, F32) for i in range(2)]
    cnt = {"kv": 0, "m": 0, "s": 0, "e": 0, "p": 0, "o": 0}

    def kvf_rows(r, i):
        r0 = ((i // 4) * 2 + r) * 512 + (i % 4) * 128
        return kvf[r0:r0 + 128, :]

    def load_kv(kvh, mode):
        s = cnt["kv"] % 2
        cnt["kv"] += 1
        ks, vs = kseq[s], vseq[s]
        if mode == "full":
            for r in range(2):
                P.dma("sp", ks.ap[:, r * 1024:(r + 1) * 1024], kvf_rows(r, kvh), [dbuf("kvf", l, r, kvh)], [ks])
                P.dma("sp", vs.ap[:, r * 8:(r + 1) * 8, :], kvf_rows(r, nkv + kvh).rearrange("p (b d) -> p b d", d=128),
                      [dbuf("kvf", l, r, nkv + kvh)], [vs])
        else:
            hb = mode
            hw = hb * 128
            P.dma("sp", ks.ap[:, 0:hw], kvf_rows(0, kvh)[:, 1024 - hw:1024], [dbuf("kvf", l, 0, kvh)], [ks])
            P.dma("sp", ks.ap[:, hw:hw + 1024], kv[kvh * 128:(kvh + 1) * 128, :], [dbuf("kv", l, kvh)], [ks])
            P.dma("sp", ks.ap[:, hw + 1024:2 * hw + 1024], kvf_rows(1, kvh)[:, 0:hw], [dbuf("kvf", l, 1, kvh)], [ks])
            vv = lambda ap: ap.rearrange("p (b d) -> p b d", d=128)
            P.dma("sp", vs.ap[:, 0:hb, :], vv(kvf_rows(0, nkv + kvh))[:, 8 - hb:8, :], [dbuf("kvf", l, 0, nkv + kvh)], [vs])
            r0 = (nkv + kvh) * 128
            P.dma("sp", vs.ap[:, hb:hb + 8, :], vv(kv[r0:r0 + 128, :]), [dbuf("kv", l, nkv + kvh)], [vs])
            P.dma("sp", vs.ap[:, hb + 8:2 * hb + 8, :], vv(kvf_rows(1, nkv + kvh))[:, 0:hb, :], [dbuf("kvf", l, 1, nkv + kvh)], [vs])
        return ks, vs

    jobs = []
    kvgroups = []

    allg = []

    def add_job(hq, qc, blocks, kvg, sink_col):
        jobs.append(dict(hq=hq, qc=qc, blocks=blocks, kvg=kvg, sink=sink_col, mt={}))
        return jobs[-1]

    if kind == "ab":
        tA, tB = D["tabA"], D["tabB"]
        for h in range(8):
            kvgroups.append((h, "full"))
            for qc in range(2):
                j = add_job(h, qc, list(range(16)), len(kvgroups) - 1, None)
                for g in range(2):
                    lo = qc * 512 - (8 * g + 7) * 128 + 1920
                    allg.append(dict(src=tA[:, h * A_TAB + lo:h * A_TAB + lo + 1408], n=1408,
                                     ent=[(j, i, (qc * 512 - i * 128 + 1920) - lo) for i in range(8 * g, 8 * g + 8)]))
        for hb in range(8):
            if hb % 4 == 0:
                kvgroups.append((8 + hb // 4, 1))
            for qc in range(2):
                c0 = (hb * 2 + qc) * 3072
                j = add_job(8 + hb, qc, [qc * 4 + i for i in range(6)], len(kvgroups) - 1, li * 8 + hb)
                for g in range(2):
                    allg.append(dict(src=tB[:, c0 + g * 1536:c0 + (g + 1) * 1536], n=1536,
                                     ent=[(j, 3 * g + k, k * 512) for k in range(3)]))
    else:
        tF = D["tabF%d" % li]
        P.dma("pool", arow_sb.ap[0:8, :], D["arow"], [], [arow_sb])
        for h in range(16):
            kvgroups.append((h, 2))
            ent = []
            for qc in range(2):
                j = add_job(h, qc, [qc * 4 + i for i in range(8)], len(kvgroups) - 1, None)
                j["rowmask"] = True
                ent += [(j, i, (14 - 2 * i) * 64) for i in range(8)]
            allg.append(dict(src=tF[:, h * C_TAB:(h + 1) * C_TAB], n=C_TAB, ent=ent))

    kvres = {}
    kvfirst = {}
    for j in jobs:
        kvfirst.setdefault(j["kvg"], id(j))
    kvfirst = {v: k for k, v in kvfirst.items()}

    def load_kvgroup(k):
        kvres[k] = load_kv(*kvgroups[k])

    tasks = [(j, i) for j in jobs for i in range(len(j["blocks"]))]
    sbank = {}
    gstart = {}
    for k, g in enumerate(allg):
        gstart[(id(g["ent"][0][0]), g["ent"][0][1])] = k

    def load_group(k):
        g = allg[k]
        m = msk[k % 2]
        P.dma("sp", m.ap[:, 0:g["n"]], g["src"], [], [m])
        for (j, i, off) in g["ent"]:
            j["mt"][i] = (m, off)

    def S(t):
        j, i = tasks[t]
        if i == 0:
            j["ks"], j["vs"] = kvres[j["kvg"]]
            par = cnt["o"] % 2
            cnt["o"] += 1
            j["bo"], j["bd"], j["r"] = P.bank(4 + par), P.bank(6 + par), rd[par]
        bk = P.bank(cnt["s"] % 4)
        cnt["s"] += 1
        sbank[t] = bk
        b = j["blocks"][i]
        ks, qb = j["ks"], q[j["hq"]][j["qc"]]
        rm = j.get("rowmask", False)
        P.pe(lambda e: e.matmul(bk.ap, lhsT=ks.ap[:, b * 128:(b + 1) * 128], rhs=qb.ap, start=True, stop=not rm),
             [ks, qb], [bk])
        if rm:
            a0 = (j["qc"] * 8 + i) * 128
            P.pe(lambda e: e.matmul(bk.ap, lhsT=arow_sb.ap[0:8, a0:a0 + 128], rhs=C.bfix.ap[0:8, :],
                                    start=False, stop=True), [arow_sb, C.bfix], [bk])

    def PV(t):
        j, i = tasks[t]
        nb = len(j["blocks"])
        k = gstart.get((id(j), i))
        if k is not None and k + 1 < len(allg):
            load_group(k + 1)
        if i == 0 and id(j) in kvfirst and kvfirst[id(j)] + 1 < len(kvgroups):
            load_kvgroup(kvfirst[id(j)] + 1)
        bk = sbank.pop(t)
        b = j["blocks"][i]
        tb = Tb[cnt["e"] % 4]
        cnt["e"] += 1
        pt = Pt[cnt["p"] % 4]
        cnt["p"] += 1
        mb, off = j["mt"][i]
        vs, bo, bd = j["vs"], j["bo"], j["bd"]
        P.dve(lambda e: e.scalar_tensor_tensor(out=tb.ap, in0=bk.ap, scalar=SCALE, in1=mb.ap[:, off:off + 512],
                                               op0=ALU.mult, op1=ALU.add), [bk, mb], [tb])
        P.act(lambda e: e.activation(out=pt.ap, in_=tb.ap, func=AF.Exp), [tb], [pt])
        P.pe(lambda e: e.matmul(bo.ap, lhsT=vs.ap[:, b, :], rhs=pt.ap, start=(i == 0), stop=(i == nb - 1)),
             [vs, pt], [bo])
        P.pe(lambda e: e.matmul(bd.ap, lhsT=C.ones_b.ap, rhs=pt.ap, start=(i == 0), stop=(i == nb - 1)),
             [C.ones_b, pt], [bd])
        if i == nb - 1:
            pending.append((t + 2, j))
        while pending and pending[0][0] <= t:
            epilogue(pending.pop(0)[1])

    pending = []

    def epilogue(j):
        r, qb, bo, bd = j["r"], q[j["hq"]][j["qc"]], j["bo"], j["bd"]
        if j["sink"] is None:
            P.act(lambda e: e.activation(out=r.ap, in_=bd.ap, func=AF.Ln), [bd], [r])
        else:
            sc = C.sink.ap[:, j["sink"]:j["sink"] + 1]
            P.act(lambda e: e.activation(out=r.ap, in_=bd.ap, func=AF.Ln, bias=sc, scale=1.0), [bd, C.sink], [r])
        P.act(lambda e: e.activation(out=r.ap, in_=r.ap, func=AF.Exp, scale=-1.0), [r], [r])
        P.dve(lambda e: e.tensor_tensor(out=qb.ap, in0=bo.ap, in1=r.ap, op=ALU.mult), [bo, r], [qb])

    LA = 3
    load_kvgroup(0)
    load_group(0)
    for t in range(min(LA, len(tasks))):
        S(t)
    for t in range(len(tasks)):
        if t + LA < len(tasks):
            S(t + LA)
        PV(t)
    while pending:
        epilogue(pending.pop(0)[1])

    def o_evac(j, th, bk):
        xb = C.x[j][th]
        P.dve(lambda e: e.tensor_tensor(out=xb.ap, in0=bk.ap, in1=xb.ap, op=ALU.add), [bk, xb], [xb])
    proj_fm(P, C, Wout, 0, D_MODEL, q, o_evac, st)


def conv_chunks(D):
    out = []
    for (tn, mn, ncols, key) in (("tabA", "mA", 8 * A_TAB, ("mA",)), ("tabB", "mB", 16 * 3072, ("mB",)),
                                 ("tabC0", "mC0", 32 * 4096, ("mC", 0)), ("tabC1", "mC1", 32 * 4096, ("mC", 1))):
        if tn not in D:
            continue
        c = 0
        while c < ncols:
            n = min(1024, ncols - c)
            out.append((D[tn][:, c:c + n], D[mn][:, c:c + n], n, key))
            c += n
    return out


def conv_burst(P, C, idxs):
    idxs = list(idxs)

    def abuf(idx):
        return P.sb(R_OFF + 20480 + (idx % 4) * 4096, 4096, F32)

    def load(idx):
        src, dst, n, key = C.bgq[idx]
        a = abuf(idx)
        P.dma("sp", a.ap[:, 0:n], src, [], [a])

    for k in idxs[:3]:
        load(k)
    for pos, idx in enumerate(idxs):
        if pos + 3 < len(idxs):
            load(idxs[pos + 3])
        src, dst, n, key = C.bgq[idx]
        a = abuf(idx)
        o = P.sb(R_OFF + 36864 + (idx % 2) * 2048, 2048, BF16)
        P.act(lambda e, a=a, o=o, n=n: e.activation(out=o.ap[:, 0:n], in_=a.ap[:, 0:n], func=AF.Exp), [a], [o])
        P.dma("sp", dst, o.ap[:, 0:n], [o], [Buf(None, [("d",) + key])])


def bg_step(P, C, n):
    hi = min(len(C.bgq), C.bgi + n)
    if hi > C.bgi:
        conv_burst(P, C, range(C.bgi, hi))
        C.bgi = hi


def bg_drain(P, C, keys):
    hi = C.bgi
    while hi < len(C.bgq) and C.bgq[hi][3] in keys:
        hi += 1
    if hi > C.bgi:
        conv_burst(P, C, range(C.bgi, hi))
        C.bgi = hi


def declare_dram(nc, segs, nl=DEPTH):
    D = {}

    def inp(name, shape, dt=F32):
        D[name] = nc.dram_tensor(name, list(shape), dt, kind="ExternalInput").ap()

    def internal(name, shape, dt):
        D[name] = nc.dram_tensor(name, list(shape), dt).ap()

    layers = sorted({s[1] for s in segs if s[0] in ("ffn", "mix")})
    mix_layers = sorted({s[1] for s in segs if s[0] == "mix"})
    inp("xT_in", (D_MODEL, T))
    inp("gains", (128, N_GAIN * KD))
    inp("sink", (128, 16))
    if any(s[0] == "ffn" for s in segs):
        inp("ffn_w_gate", (nl, 2, D_MODEL, D_FF))
        inp("ffn_w_up", (nl, 2, D_MODEL, D_FF))
        inp("ffn_w_down", (nl, 2, D_FF, D_MODEL))
    nab = (nl + 1) // 2
    ncl = nl // 2
    if any(l % 2 == 0 for l in mix_layers):
        inp("ab_w_in", (nab, D_MODEL, 4608))
        inp("ab_w_out", (nab, D_MODEL, D_MODEL))
        inp("tabA", (128, 8 * A_TAB))
        inp("tabB", (128, 16 * 3072))
    if any(l % 2 == 1 for l in mix_layers):
        inp("c_w_in", (max(ncl, 1), D_MODEL, 6144))
        inp("c_w_out", (max(ncl, 1), D_MODEL, D_MODEL))
        for l in mix_layers:
            if l % 2 == 1:
                inp("tabF%d" % (l // 2), (128, 16 * C_TAB))
        inp("arow", (8, 2048))
        inp("bfix", (8, 512))
    for l in mix_layers:
        nkv = MIX["ab" if l % 2 == 0 else "c"]["nkv"]
        internal("kv%d" % l, (2 * nkv * 128, T), BF16)
        internal("kvf%d" % l, (2 * 2 * nkv * 128, T), BF16)
    D["xT_out"] = nc.dram_tensor("xT_out", [D_MODEL, T], F32, kind="ExternalOutput").ap()
    return D


def build(segs, nl=DEPTH, groups=None):
    if groups is None:
        groups = [[2 * i, 2 * i + 1] for i in range(NCORES // 2)]
    nc = bass.Bass("TRN2", target_bir_lowering=False)
    D = declare_dram(nc, segs, nl)
    with ExitStack() as es:
        P = Prog(nc, es, SB_BYTES)
        C = Ctx()
        setup_common(P, C)

        def body():
            C.bgi = 0
            C.bgl = 0
            emit_consts(P, C, D)
            emit_load_x(P, C, D["xT_in"])
            for s in segs:
                if s[0] == "ffn":
                    emit_ffn(P, C, D, s[1], s[2])
                elif s[0] == "mix":
                    emit_mixer(P, C, D, s[1], groups)
                elif s[0] == "final":
                    emit_rmsnorm(P, C, DEPTH * 3, final=True)
            emit_store_x(P, C, D["xT_out"])

        P.dry = True
        body()
        P.dry = False
        body()
        block = es.enter_context(nc.Block())
        P.finalize(block)
    return nc


def _alibi(n):
    return 2.0 ** (-8.0 * np.arange(1, n + 1, dtype=np.float64) / n)


def make_tab_a(half):
    qoff = half * T
    p = np.arange(128)[:, None]
    j = np.arange(A_TAB)[None, :]
    d = j - p - 1920 + qoff
    ad = np.abs(d)
    mult = (ad <= 64).astype(np.float64) + ((d % 4 == 0) & (ad <= 256)) + ((d % 16 == 0) & (ad <= 1024))
    sl = _alibi(8)
    out = np.empty((128, 8, A_TAB), np.float32)
    with np.errstate(divide="ignore"):
        lm = np.where(mult > 0, np.log(np.maximum(mult, 1e-30)), 0.0)
    for h in range(8):
        out[:, h, :] = np.where(mult > 0, -sl[h] * ad + lm, NEG).astype(np.float32)
    return np.ascontiguousarray(out.reshape(128, 8 * A_TAB))


def make_tab_b(half):
    qoff = half * T
    sl = _alibi(8)
    out = np.empty((128, 8, 2, 6, 512), np.float32)
    p = np.arange(128)[:, None]
    qq = np.arange(512)[None, :]
    for qc in range(2):
        for blk in range(6):
            k_rel = (qc * 4 + blk) * 128 + p - 128
            q_rel = qc * 512 + qq
            d = np.abs(q_rel - k_rel)
            kg = qoff + k_rel
            valid = (d <= 128) & (kg >= 0) & (kg < SEQ)
            for h in range(8):
                out[:, h, qc, blk, :] = np.where(valid, -sl[h] * d, NEG).astype(np.float32)
    return np.ascontiguousarray(out.reshape(128, 16 * 3072))


def make_tab_c(half, rpb):
    qoff = half * T
    out = np.empty((128, 16, 2, 8, 512), np.float32)
    p = np.arange(128)[:, None]
    qq = np.arange(512)[None, :]
    for qc in range(2):
        qg = qoff + qc * 512 + qq
        r, c = qg // 64, qg % 64
        rs = np.clip(r - 4, 0, 24)
        cs = np.clip(c - 8, 0, 48)
        for blk in range(8):
            kg = qoff + (qc * 4 + blk) * 128 + p - 256
            kr, kc = kg // 64, kg % 64
            valid = (kg >= 0) & (kg < SEQ) & (kr >= rs) & (kr < rs + 8) & (kc >= cs) & (kc < cs + 16)
            ri = np.clip(kr - r + 7, 0, 14)
            ci = np.clip(kc - c + 15, 0, 30)
            g = rpb[:, ri, ci]
            out[:, :, qc, blk, :] = np.where(valid[None], g, np.float32(NEG)).transpose(1, 0, 2)
    return np.ascontiguousarray(out.reshape(128, 32 * 4096))


C_TAB = 22 * 64


def make_tab_c2(half, rpb):
    qoff = half * T
    p = np.arange(128)
    hf, kc = p // 64, p % 64
    u = np.arange(22)
    c = np.arange(64)
    dr = hf[:, None] + 10 - u[None, :]
    cs = np.clip(c - 8, 0, 48)
    colv = (kc[:, None] >= cs[None, :]) & (kc[:, None] < cs[None, :] + 16)
    dc = kc[:, None] - c[None, :]
    ri = np.clip(dr + 7, 0, 14)
    ci = np.clip(dc + 15, 0, 30)
    valid = (np.abs(dr) <= 7)[:, :, None] & colv[:, None, :]
    g = rpb[:, ri[:, :, None], ci[:, None, :]]
    F = np.where(valid[None], g, np.float32(NEG)).astype(np.float32).transpose(1, 0, 2, 3)
    arow = np.empty((8, 2, 8, 128), np.float32)
    for qc in range(2):
        for blk in range(8):
            kg0 = qoff + (qc * 4 + blk) * 128 - 256
            kr = kg0 // 64 + hf
            r = (qoff + qc * 512) // 64 + np.arange(8)
            rs = np.clip(r - 4, 0, 24)
            ok = (kr[None, :] >= 0) & (kr[None, :] < 32) & (kr[None, :] >= rs[:, None]) & (kr[None, :] < rs[:, None] + 8)
            arow[:, qc, blk, :] = np.where(ok, 0.0, NEG)
    return np.ascontiguousarray(F.reshape(128, 16 * C_TAB)), np.ascontiguousarray(arow.reshape(8, 2048))


def make_bfix():
    b = np.zeros((8, 8, 64), np.float32)
    for r in range(8):
        b[r, r, :] = 1.0
    return b.reshape(8, 512)


def fm(v):
    v = np.asarray(v, np.float32).reshape(-1, KD, 128)
    return np.ascontiguousarray(v.transpose(2, 0, 1).reshape(128, -1))


FULL_SEGS = []
for _l in range(DEPTH):
    FULL_SEGS += [("ffn", _l, 0), ("mix", _l), ("ffn", _l, 1)]
FULL_SEGS.append(("final",))


def kernel(x, ffn_norm, ffn_w_gate, ffn_w_up, ffn_w_down, mix_norm, ab_w_in, ab_w_out,
           ab_sink, c_w_in, c_w_out, c_rpb, final_norm):
    x = np.asarray(x, np.float32)
    gl = []
    for l in range(DEPTH):
        gl += [ffn_norm[l, 0], mix_norm[l], ffn_norm[l, 1]]
    gl.append(final_norm)
    gains = fm(np.stack([np.asarray(g, np.float32) for g in gl]))
    sink = np.ascontiguousarray(np.broadcast_to(np.asarray(ab_sink, np.float32).reshape(1, 16), (128, 16)))
    rpb = np.asarray(c_rpb, np.float32)
    tabs = []
    for half in range(2):
        f0, ar = make_tab_c2(half, rpb[0])
        f1, _ = make_tab_c2(half, rpb[1])
        tabs.append(dict(tabA=make_tab_a(half), tabB=make_tab_b(half), tabF0=f0, tabF1=f1, arow=ar,
                         bfix=make_bfix()))
    shared = dict(gains=gains, sink=sink,
                  ffn_w_gate=np.asarray(ffn_w_gate, np.float32), ffn_w_up=np.asarray(ffn_w_up, np.float32),
                  ffn_w_down=np.asarray(ffn_w_down, np.float32), ab_w_in=np.asarray(ab_w_in, np.float32),
                  ab_w_out=np.asarray(ab_w_out, np.float32), c_w_in=np.asarray(c_w_in, np.float32),
                  c_w_out=np.asarray(c_w_out, np.float32))
    in_maps = []
    for c in range(NCORES):
        b, half = c // 2, c % 2
        m = dict(shared)
        m.update(tabs[half])
        m["xT_in"] = np.ascontiguousarray(x[b, half * T:(half + 1) * T, :].T)
        in_maps.append(m)
    nc = build(FULL_SEGS)
    res = run_bass_kernel_spmd(nc, in_maps, core_ids=list(range(NCORES)))
    out = np.empty((BATCH, SEQ, D_MODEL), np.float32)
    for c in range(NCORES):
        b, half = c // 2, c % 2
        out[b, half * T:(half + 1) * T, :] = res.results[c]["xT_out"].T
    return out
```
